# Optimizing a Trainium2 kernel written in Bass

```python
import jax, jax.numpy as jnp
from jax import lax
import numpy as np

D_MODEL = 1024
BATCH = 16
SEQ = 256
DEPTH = 4
DEC_BATCH = 2
DEC_SEQ = 2048
PAST_LEN = 512

GRID_W = 64
HEAD_DIM = 64
MIX_W = D_MODEL
A_W = MIX_W // 4
B_W = MIX_W // 2
C_W = MIX_W // 4
A_HEADS = A_W // HEAD_DIM
B_HEADS = B_W // HEAD_DIM
B_KV = B_HEADS // 4
B_GROUP = B_HEADS // B_KV
C_HEADS = C_W // HEAD_DIM
LORA_W = 64
LORA_A = 64
LORA_G = 128
RW_W = 3 * A_W + LORA_W + LORA_A + LORA_G
AT_W = B_W + 2 * B_KV * HEAD_DIM
DN_W = 4 * C_W + 4 * C_HEADS
PROJ_W = RW_W + AT_W + DN_W
WINDOW = 128
BLOCK = 128
CHUNK = 64
CONV_K = 5
ROPE_BASE = 10000.0
ROT_HALF = HEAD_DIM // 2
ROT_PAIRS = ROT_HALF // 2
D_FF = 2816
N_MOD = 9
EPS = 1e-6
LNX_EPS = 64e-5
ATTN_SCALE = HEAD_DIM ** -0.5
NEG_INF = -1e30

kernel_name = 'hybrid_dit_rwkv7_swa_gdn_step'


def rms_norm(x, w):
    xf = x.astype(jnp.float32)
    y = xf * lax.rsqrt(jnp.mean(xf * xf, axis=-1, keepdims=True) + EPS)
    return y.astype(x.dtype) * w


def l2_normalize(x):
    xf = x.astype(jnp.float32)
    return xf * lax.rsqrt(jnp.sum(xf * xf, axis=-1, keepdims=True) + EPS)


def modulate(x, w, shift, scale):
    return rms_norm(x, w) * (1 + scale) + shift


def swiglu(x, w_in, w_out):
    gate, up = jnp.split(x @ w_in, 2, axis=-1)
    return (jax.nn.silu(gate) * up) @ w_out


def token_shift(x, mu):
    xp = jnp.pad(x, ((0, 0), (1, 1), (0, 0)))
    return x + mu * (0.5 * (xp[:, :-2] + xp[:, 2:]) - x)


def short_conv(x, w):
    return lax.conv_general_dilated(x, w[:, None, :].astype(x.dtype), window_strides=(1,),
                                    padding=[(CONV_K // 2, CONV_K // 2)],
                                    dimension_numbers=('NWC', 'WIO', 'NWC'),
                                    feature_group_count=x.shape[-1])


def axial_rope(x, rows):
    row = jnp.repeat(jnp.arange(rows), GRID_W)
    col = jnp.tile(jnp.arange(GRID_W), rows)
    freqs = ROPE_BASE ** (-jnp.arange(ROT_PAIRS, dtype=jnp.float32) / ROT_PAIRS)

    def rotate(xh, pos):
        shape = (pos.shape[0],) + (1,) * (xh.ndim - 3) + (1,)
        ang = pos.astype(jnp.float32).reshape(shape) * freqs
        cos, sin = jnp.cos(ang), jnp.sin(ang)
        x1, x2 = xh[..., :ROT_PAIRS], xh[..., ROT_PAIRS:]
        return jnp.concatenate([x1 * cos - x2 * sin, x2 * cos + x1 * sin], axis=-1)

    xf = x.astype(jnp.float32)
    return jnp.concatenate([rotate(xf[..., :ROT_HALF], row), rotate(xf[..., ROT_HALF:], col)], axis=-1).astype(x.dtype)


def rwkv7_scan(r, w, k, v, a, b, s0):
    def step(S, inp):
        r_t, w_t, k_t, v_t, a_t, b_t = inp
        S = (S * w_t[:, :, None, :]
             + jnp.einsum('bhvk,bhk->bhv', S, a_t)[..., None] * b_t[:, :, None, :]
             + v_t[..., None] * k_t[:, :, None, :])
        return S, jnp.einsum('bhvk,bhk->bhv', S, r_t)

    s_fin, o = lax.scan(step, s0.astype(jnp.float32), tuple(jnp.swapaxes(t, 0, 1) for t in (r, w, k, v, a, b)))
    return jnp.swapaxes(o, 0, 1), s_fin


def rwkv_mixer(rw, s0_f, s0_b, lp):
    B, T, _ = rw.shape
    rw = token_shift(rw, lp['rwkv_mu'])
    r, k, v, xw, xa, xg = jnp.split(rw, [A_W, 2 * A_W, 3 * A_W, 3 * A_W + LORA_W, 3 * A_W + LORA_W + LORA_A], axis=-1)
    heads = lambda t: t.astype(jnp.float32).reshape(B, T, A_HEADS, HEAD_DIM)
    g = jax.nn.sigmoid(xg) @ lp['rwkv_g2']
    kk = l2_normalize(heads(k * lp['rwkv_kk']))
    rh, vh = heads(r), heads(v)
    rk = lp['rwkv_rk'].astype(jnp.float32).reshape(A_HEADS, HEAD_DIM)
    outs, bonus, finals = [], [], []
    for d, s0 in enumerate((s0_f, s0_b)):
        w = -jax.nn.softplus(-(lp['rwkv_w0'][d] + jnp.tanh(xw) @ lp['rwkv_w2'][d])) - 0.5
        a = jax.nn.sigmoid(lp['rwkv_a0'][d] + xa @ lp['rwkv_a2'][d])
        kd = heads(k * (1 + (a - 1) * lp['rwkv_ka']))
        ins = (rh, jnp.exp(-jnp.exp(heads(w))), kd, vh, -kk, kk * heads(a))
        if d == 1:
            ins = tuple(t[:, ::-1] for t in ins)
        o, s_fin = rwkv7_scan(*ins, s0)
        outs.append(o if d == 0 else o[:, ::-1])
        bonus.append(jnp.sum(rh * kd * rk, axis=-1, keepdims=True) * vh)
        finals.append(s_fin)
    o = outs[0] + outs[1]
    mean = jnp.mean(o, axis=-1, keepdims=True)
    var = jnp.mean(jnp.square(o - mean), axis=-1, keepdims=True)
    y = ((o - mean) * lax.rsqrt(var + LNX_EPS)).reshape(B, T, A_W) * lp['rwkv_lnx_w'] + lp['rwkv_lnx_b']
    y = (y + (bonus[0] + bonus[1]).reshape(B, T, A_W)) * g
    return y.astype(rw.dtype), finals[0], finals[1]


def attn_heads(at):
    B, T, _ = at.shape
    q, k, v = jnp.split(at, [B_W, B_W + B_KV * HEAD_DIM], axis=-1)
    return (q.reshape(B, T, B_KV, B_GROUP, HEAD_DIM),
            k.reshape(B, T, B_KV, HEAD_DIM),
            v.reshape(B, T, B_KV, HEAD_DIM))


def sink_attend(q, key_sets, sink):
    scores = []
    for k, _, mask in key_sets:
        s = jnp.einsum('bqhgd,bkhd->bhgqk', q, k).astype(jnp.float32) * ATTN_SCALE
        scores.append(s if mask is None else jnp.where(mask, s, NEG_INF))
    sink_col = jnp.broadcast_to(sink.astype(jnp.float32).reshape(B_KV, B_GROUP, 1, 1), scores[0].shape[:-1] + (1,))
    p = jax.nn.softmax(jnp.concatenate(scores + [sink_col], axis=-1), axis=-1)
    outs, start = [], 0
    for (_, v, _), s in zip(key_sets, scores):
        n = s.shape[-1]
        outs.append(jnp.einsum('bhgqk,bkhd->bqhgd', p[..., start:start + n].astype(v.dtype), v))
        start += n
    return sum(outs[1:], outs[0])


def context_attention(q, k, v, sink):
    B, T = q.shape[:2]
    qb = jnp.moveaxis(q.reshape(B, T // BLOCK, BLOCK, B_KV, B_GROUP, HEAD_DIM), 1, 0)
    o = lax.map(lambda qi: sink_attend(qi, ((k, v, None),), sink), qb)
    return jnp.moveaxis(o, 0, 1).reshape(B, T, B_W)


def latent_attention(q, k, v, ctx_k, ctx_v, sink):
    B, T = q.shape[:2]
    nb = T // BLOCK
    qb = jnp.moveaxis(q.reshape(B, nb, BLOCK, B_KV, B_GROUP, HEAD_DIM), 1, 0)

    def neighbours(t):
        tp = jnp.pad(t, ((0, 0), (BLOCK, BLOCK), (0, 0), (0, 0))).reshape(B, nb + 2, BLOCK, B_KV, HEAD_DIM)
        return jnp.moveaxis(jnp.concatenate([tp[:, :-2], tp[:, 1:-1], tp[:, 2:]], axis=2), 1, 0)

    qi = jnp.arange(BLOCK)[:, None]
    kj = jnp.arange(3 * BLOCK)[None, :]
    key_pos = (jnp.arange(nb)[:, None, None] - 1) * BLOCK + kj[None]
    mask = (jnp.abs(kj - BLOCK - qi) <= WINDOW)[None] & (key_pos >= 0) & (key_pos < T)
    o = lax.map(lambda xs: sink_attend(xs[0], ((xs[1], xs[2], xs[3]), (ctx_k, ctx_v, None)), sink),
                (qb, neighbours(k), neighbours(v), mask))
    return jnp.moveaxis(o, 0, 1).reshape(B, T, B_W)


def gated_delta_chunked(q, k, v, g, beta, s0):
    B, T, H, _ = q.shape
    n = T // CHUNK

    def chunks(t):
        return jnp.moveaxis(t.astype(jnp.float32).reshape(B, n, CHUNK, H, -1), 3, 2)

    qc, kc, vc = chunks(q), chunks(k), chunks(v)
    gc = chunks(g[..., None])[..., 0]
    bc = chunks(beta[..., None])[..., 0]
    gcum = jnp.cumsum(gc, axis=-1)
    tril = jnp.tril(jnp.ones((CHUNK, CHUNK), dtype=bool))
    gdiff = gcum[..., :, None] - gcum[..., None, :]
    decay = jnp.where(tril, jnp.exp(jnp.where(tril, gdiff, 0.0)), 0.0)
    kb = kc * bc[..., None]
    lower = jnp.einsum('bnhid,bnhjd->bnhij', kb, kc) * decay
    rhs = jnp.concatenate([vc * bc[..., None], kb * jnp.exp(gcum)[..., None]], axis=-1)
    sol = lax.linalg.triangular_solve(lower, rhs, left_side=True, lower=True, unit_diagonal=True)
    u_in, w_cum = sol[..., :HEAD_DIM], sol[..., HEAD_DIM:]
    qk = jnp.einsum('bnhid,bnhjd->bnhij', qc, kc) * decay
    q_dec = qc * jnp.exp(gcum)[..., None]
    k_dec = kc * jnp.exp(gcum[..., -1:] - gcum)[..., None]
    g_tot = jnp.exp(gcum[..., -1])

    def step(S, xs):
        u_i, w_i, qk_i, qd_i, kd_i, gt_i = xs
        u = u_i - jnp.einsum('bhcd,bhde->bhce', w_i, S)
        o = jnp.einsum('bhcd,bhde->bhce', qd_i, S) + jnp.einsum('bhij,bhje->bhie', qk_i, u)
        S = S * gt_i[..., None, None] + jnp.einsum('bhcd,bhce->bhde', kd_i, u)
        return S, o

    xs = tuple(jnp.moveaxis(t, 1, 0) for t in (u_in, w_cum, qk, q_dec, k_dec, g_tot))
    s_fin, o = lax.scan(step, s0.astype(jnp.float32), xs)
    o = jnp.moveaxis(jnp.moveaxis(o, 0, 1), 2, 3).reshape(B, T, H, -1)
    return o, s_fin


def deltanet_mixer(dn, s0_f, s0_b, lp):
    B, T, _ = dn.shape
    qkv, gate, ab = jnp.split(dn, [3 * C_W, 4 * C_W], axis=-1)
    qkv = jax.nn.silu(short_conv(qkv, lp['dn_conv']))
    q, k, v = (t.reshape(B, T, C_HEADS, HEAD_DIM) for t in jnp.split(qkv, 3, axis=-1))
    q = l2_normalize(q) * ATTN_SCALE
    k = l2_normalize(k)
    ab = ab.astype(jnp.float32).reshape(B, T, 4, C_HEADS)
    outs, finals = [], []
    for d, s0 in enumerate((s0_f, s0_b)):
        g = -jnp.exp(lp['dn_A_log'][d]) * jax.nn.softplus(ab[:, :, d] + lp['dn_dt_bias'][d])
        beta = jax.nn.sigmoid(ab[:, :, 2 + d])
        ins = (q, k, v, g, beta)
        if d == 1:
            ins = tuple(t[:, ::-1] for t in ins)
        o, s_fin = gated_delta_chunked(*ins, s0)
        outs.append(o if d == 0 else o[:, ::-1])
        finals.append(s_fin)
    o = outs[0] + outs[1]
    o = o * lax.rsqrt(jnp.mean(o * o, axis=-1, keepdims=True) + EPS) * lp['dn_norm_w']
    o = o * jax.nn.silu(gate.astype(jnp.float32).reshape(B, T, C_HEADS, HEAD_DIM))
    return o.reshape(B, T, C_W).astype(dn.dtype), finals[0], finals[1]


def mix_context(h, lp):
    B = h.shape[0]
    rw, at, dn = jnp.split(h @ lp['w_in'], [RW_W, RW_W + AT_W], axis=-1)
    q, k, v = attn_heads(at)
    y_b = context_attention(q, k, v, lp['attn_sink'])
    za = jnp.zeros((B, A_HEADS, HEAD_DIM, HEAD_DIM), jnp.float32)
    y_a, ra_f, ra_b = rwkv_mixer(rw, za, za, lp)
    zc = jnp.zeros((B, C_HEADS, HEAD_DIM, HEAD_DIM), jnp.float32)
    y_c, dn_f, dn_b = deltanet_mixer(dn, zc, zc, lp)
    y = jnp.concatenate([y_a, y_b, y_c], axis=-1) @ lp['w_out']
    return y, (k, v, jnp.stack([ra_f, ra_b], axis=1), jnp.stack([dn_f, dn_b], axis=1))


def mix_latent(h, lp, ctx_k, ctx_v, st_rwkv, st_delta):
    T = h.shape[1]
    rows = T // GRID_W
    rw, at, dn = jnp.split(h @ lp['w_in'], [RW_W, RW_W + AT_W], axis=-1)
    q, k, v = attn_heads(at)
    q, k = axial_rope(q, rows), axial_rope(k, rows)
    y_b = latent_attention(q, k, v, ctx_k, ctx_v, lp['attn_sink'])
    y_a, _, _ = rwkv_mixer(rw, st_rwkv[:, 0], st_rwkv[:, 1], lp)
    y_c, _, _ = deltanet_mixer(dn, st_delta[:, 0], st_delta[:, 1], lp)
    y = jnp.concatenate([y_a, y_b, y_c], axis=-1) @ lp['w_out']
    return y, ()


def trunk_layer(x, cond, lp, mixer, *mixer_args):
    m = (jax.nn.silu(cond) @ lp['ada_w'] + lp['ada_b']).reshape(cond.shape[0], 1, N_MOD, D_MODEL)
    mod = lambda i: m[:, :, i]
    x = x + 0.5 * mod(2) * swiglu(modulate(x, lp['norm_w'][0], mod(0), mod(1)), lp['ffn_w_in'][0], lp['ffn_w_out'][0])
    y, st = mixer(modulate(x, lp['norm_w'][1], mod(3), mod(4)), lp, *mixer_args)
    x = x + mod(5) * y
    x = x + 0.5 * mod(8) * swiglu(modulate(x, lp['norm_w'][2], mod(6), mod(7)), lp['ffn_w_in'][1], lp['ffn_w_out'][1])
    return x, st


def setup_inputs(seed: int = 0) -> dict:
    key = jax.random.key(seed)
    ks = iter(jax.random.split(key, 40))

    def nrm(shape, scale):
        return scale * jax.random.normal(next(ks), shape, jnp.float32)

    def unif(shape, lo, hi):
        return jax.random.uniform(next(ks), shape, jnp.float32, lo, hi)

    dt = jnp.exp(unif((DEPTH, 2, C_HEADS), float(np.log(1e-3)), float(np.log(1e-1))))
    return {
        'x_prompt': nrm((BATCH, SEQ, D_MODEL), 1.0),
        'x_sample': nrm((DEC_BATCH, DEC_SEQ, D_MODEL), 1.0),
        'cache_attn_k': nrm((DEC_BATCH, DEPTH, PAST_LEN, B_KV, HEAD_DIM), 1.0),
        'cache_attn_v': nrm((DEC_BATCH, DEPTH, PAST_LEN, B_KV, HEAD_DIM), 1.0),
        'state_rwkv': nrm((DEC_BATCH, DEPTH, 2, A_HEADS, HEAD_DIM, HEAD_DIM), 0.3),
        'state_delta': nrm((DEC_BATCH, DEPTH, 2, C_HEADS, HEAD_DIM, HEAD_DIM), 0.3),
        'c': nrm((DEC_BATCH, D_MODEL), 1.0),
        'c_ctx': nrm((D_MODEL,), 1.0),
        'norm_w': 1.0 + nrm((DEPTH, 3, D_MODEL), 0.05),
        'ada_w': nrm((DEPTH, D_MODEL, N_MOD * D_MODEL), 0.5 * D_MODEL ** -0.5),
        'ada_b': nrm((DEPTH, N_MOD * D_MODEL), 0.02),
        'ffn_w_in': nrm((DEPTH, 2, D_MODEL, 2 * D_FF), D_MODEL ** -0.5),
        'ffn_w_out': nrm((DEPTH, 2, D_FF, D_MODEL), D_FF ** -0.5),
        'w_in': nrm((DEPTH, D_MODEL, PROJ_W), D_MODEL ** -0.5),
        'w_out': nrm((DEPTH, MIX_W, D_MODEL), MIX_W ** -0.5),
        'rwkv_mu': unif((DEPTH, RW_W), 0.0, 1.0),
        'rwkv_w0': unif((DEPTH, 2, A_W), -5.0, 0.0),
        'rwkv_w2': nrm((DEPTH, 2, LORA_W, A_W), 0.1 * LORA_W ** -0.5),
        'rwkv_a0': nrm((DEPTH, 2, A_W), 0.5),
        'rwkv_a2': nrm((DEPTH, 2, LORA_A, A_W), 0.1 * LORA_A ** -0.5),
        'rwkv_g2': nrm((DEPTH, LORA_G, A_W), LORA_G ** -0.5),
        'rwkv_kk': 0.85 + nrm((DEPTH, A_W), 0.05),
        'rwkv_ka': 1.0 + nrm((DEPTH, A_W), 0.05),
        'rwkv_rk': nrm((DEPTH, A_W), 0.1),
        'rwkv_lnx_w': 1.0 + nrm((DEPTH, A_W), 0.05),
        'rwkv_lnx_b': nrm((DEPTH, A_W), 0.02),
        'attn_sink': nrm((DEPTH, B_HEADS), 0.5),
        'dn_conv': nrm((DEPTH, CONV_K, 3 * C_W), CONV_K ** -0.5),
        'dn_A_log': jnp.log(unif((DEPTH, 2, C_HEADS), 1.0, 16.0)),
        'dn_dt_bias': dt + jnp.log(-jnp.expm1(-dt)),
        'dn_norm_w': 1.0 + nrm((DEPTH, HEAD_DIM), 0.05),
        'final_norm_w': 1.0 + nrm((D_MODEL,), 0.05),
    }


def reference(x_prompt, x_sample, cache_attn_k, cache_attn_v, state_rwkv, state_delta, c, c_ctx,
              norm_w, ada_w, ada_b, ffn_w_in, ffn_w_out, w_in, w_out,
              rwkv_mu, rwkv_w0, rwkv_w2, rwkv_a0, rwkv_a2, rwkv_g2, rwkv_kk, rwkv_ka, rwkv_rk,
              rwkv_lnx_w, rwkv_lnx_b, attn_sink, dn_conv, dn_A_log, dn_dt_bias, dn_norm_w, final_norm_w):
    xp, xs = x_prompt, x_sample
    ks, vs, srs, sds = [], [], [], []
    for l in range(DEPTH):
        lp = {
            'norm_w': norm_w[l], 'ada_w': ada_w[l], 'ada_b': ada_b[l],
            'ffn_w_in': ffn_w_in[l], 'ffn_w_out': ffn_w_out[l],
            'w_in': w_in[l], 'w_out': w_out[l],
            'rwkv_mu': rwkv_mu[l], 'rwkv_w0': rwkv_w0[l], 'rwkv_w2': rwkv_w2[l],
            'rwkv_a0': rwkv_a0[l], 'rwkv_a2': rwkv_a2[l], 'rwkv_g2': rwkv_g2[l],
            'rwkv_kk': rwkv_kk[l], 'rwkv_ka': rwkv_ka[l], 'rwkv_rk': rwkv_rk[l],
            'rwkv_lnx_w': rwkv_lnx_w[l], 'rwkv_lnx_b': rwkv_lnx_b[l],
            'attn_sink': attn_sink[l],
            'dn_conv': dn_conv[l], 'dn_A_log': dn_A_log[l], 'dn_dt_bias': dn_dt_bias[l], 'dn_norm_w': dn_norm_w[l],
        }
        xp, (k_l, v_l, sr_l, sd_l) = trunk_layer(xp, c_ctx[None, :], lp, mix_context)
        xs, _ = trunk_layer(xs, c, lp, mix_latent, cache_attn_k[:, l], cache_attn_v[:, l], state_rwkv[:, l], state_delta[:, l])
        ks.append(k_l)
        vs.append(v_l)
        srs.append(sr_l)
        sds.append(sd_l)
    y_prompt = rms_norm(xp, final_norm_w)
    y_sample = rms_norm(xs, final_norm_w)
    new_attn_k = jnp.stack(ks, axis=1)
    new_attn_v = jnp.stack(vs, axis=1)
    new_state_rwkv = jnp.stack(srs, axis=1)
    new_state_delta = jnp.stack(sds, axis=1)
    return (y_prompt, y_sample, new_attn_k, new_attn_v, new_state_rwkv, new_state_delta)
```

```python
import numpy as np
from contextlib import ExitStack
import concourse.bass as bass
import concourse.mybir as mybir
from concourse.bass_utils import run_bass_kernel_spmd

F32 = mybir.dt.float32
BF16 = mybir.dt.bfloat16
ALU = mybir.AluOpType
AF = mybir.ActivationFunctionType

D = 1024
L = 4
DFF = 2816
NT = 2560
TT = 512
NTILE = NT // TT
PROJ_W = 2832
NPC = 23
EPS = 1e-6
N_LAYERS_BUILD = L


class FW:
    def __init__(self, nc, es, n_dma_slots=32):
        self.nc = nc
        self.eng = {'pe': nc.tensor, 'act': nc.scalar, 'dve': nc.vector, 'pool': nc.gpsimd, 'sp': nc.sync}
        self.sem = {e: es.enter_context(nc.semaphore('s_' + e)) for e in self.eng}
        self.cnt = {e: 0 for e in self.eng}
        self.seen = {e: {} for e in self.eng}
        self.dslots = [es.enter_context(nc.semaphore('d%d' % i)) for i in range(n_dma_slots)]
        self.dcnt = [0] * n_dma_slots
        self.dnext = 0
        self.dnext2 = [0, 0]
        self.lastw = {}
        self.reads = {}
        self.nins = 0
        self.rt = {}

    def _wait(self, e, prod):
        if prod is None:
            return
        kind, idx, val = prod
        if kind == 'e' and idx == e and e in ('pe', 'sp'):
            return
        k = (kind, idx)
        if self.seen[e].get(k, 0) >= val:
            return
        sem = self.sem[idx] if kind == 'e' else self.dslots[idx]
        self.eng[e].wait_ge(sem, val)
        self.seen[e][k] = val

    def _deps(self, e, reads, writes):
        for k in reads:
            self._wait(e, self.lastw.get(k))
        for k in writes:
            self._wait(e, self.lastw.get(k))
            for p in self.reads.get(k, {}).values():
                self._wait(e, p)

    def _record(self, prod, reads, writes):
        for k in reads:
            self.reads.setdefault(k, {})[(prod[0], prod[1])] = prod
        for k in writes:
            self.lastw[k] = prod
            self.reads[k] = {}

    def op(self, e, reads, writes, fn, rt=0, inc=True):
        self._deps(e, reads, writes)
        if e == 'pe':
            for k in writes:
                if self.rt.get(k, 0) != rt and self.cnt['pe']:
                    if self.seen['pe'].get(('e', 'pe'), 0) < self.cnt['pe']:
                        self.eng['pe'].wait_ge(self.sem['pe'], self.cnt['pe'])
                        self.seen['pe'][('e', 'pe')] = self.cnt['pe']
                self.rt[k] = rt
        ins = fn()
        self.nins += 1
        if inc:
            self.cnt[e] += 1
            ins.then_inc(self.sem[e], 1)
            tok = self.cnt[e]
        else:
            tok = self.cnt[e] + 1
        self._record(('e', e, tok), reads, writes)
        return ins

    def dma(self, q, out, in_, reads, writes, **kw):
        half = len(self.dslots) // 2
        qi = 1 if q == 'pool' else 0
        s = qi * half + self.dnext2[qi]
        self.dnext2[qi] = (self.dnext2[qi] + 1) % half
        self._wait(q, ('d', s, self.dcnt[s]) if self.dcnt[s] else None)
        self._deps(q, reads, writes)
        ins = self.eng[q].dma_start(out=out, in_=in_, **kw)
        self.dcnt[s] += 16
        self.nins += 1
        ins.then_inc(self.dslots[s], 16)
        self._record(('d', s, self.dcnt[s]), reads, writes)
        return ins

    def barrier(self):
        for e in self.eng:
            for f in self.eng:
                if self.cnt[f]:
                    self._wait(e, ('e', f, self.cnt[f]))
            for i in range(len(self.dslots)):
                if self.dcnt[i]:
                    self._wait(e, ('d', i, self.dcnt[i]))

    def finish(self, e='sp'):
        for k, p in list(self.lastw.items()):
            self._wait(e, p)
        for k, d in list(self.reads.items()):
            for p in d.values():
                self._wait(e, p)


class Ring:
    def __init__(self, tiles, name):
        self.tiles = tiles
        self.name = name
        self.i = 0

    def next(self):
        t = self.tiles[self.i]
        k = '%s%d' % (self.name, self.i)
        self.i = (self.i + 1) % len(self.tiles)
        return t, k


ENABLE_ATTN = True
DEBUG = False
SCAN_SUB = 99
SCAN_TEST = None
_LAST = None
ENABLE_RWKV = True
ENABLE_DN = True
NEG = -1e30


def build_program():
    nc = bass.Bass("TRN2", target_bir_lowering=False)
    din = lambda name, shape: nc.dram_tensor(name, shape, F32, kind="ExternalInput").ap()
    dout = lambda name, shape: nc.dram_tensor(name, shape, F32, kind="ExternalOutput").ap()
    xT_in = din("xT", [D, NT])
    condT = din("condT", [128, 8, 2])
    if SCAN_TEST:
        _din = din
        din = lambda name, shape: _din(name, [1, 1, 128, 128] if name in ("ffn_w_in", "ffn_w_out") else ([1, 128, 128] if name in ("ada_w", "w_in", "w_out") else shape))
    ada_w = din("ada_w", [L, D, 9 * D])
    ada_bT = din("ada_bT", [L, 128, 72])
    norm_wT = din("norm_wT", [L, 128, 24])
    fnorm_wT = din("fnorm_wT", [128, 8])
    ffn_w_in = din("ffn_w_in", [L, 2, D, 2 * DFF])
    ffn_w_out = din("ffn_w_out", [L, 2, DFF, D])
    w_in = din("w_in", [L, D, PROJ_W])
    w_out = din("w_out", [L, D, D])
    ones_c = din("ones_c", [128, 128])
    ident_c = din("ident_c", [128, 128])
    cos_c = din("cos_c", [64, 2048])
    sin_c = din("sin_c", [64, 2048])
    rotT_c = din("rotT_c", [64, 64])
    amask_c = din("amask_c", [128, 3, 384])
    sinkb = din("sinkb", [L, 128, 8])
    ck_in = din("ck", [L, 512, 128])
    cv_in = din("cv", [L, 512, 128])
    rwp_in = din("rwp", [L, 4, 128, 12])
    w2cat_in = din("w2cat", [L, 4, 64, 128])
    a2cat_in = din("a2cat", [L, 4, 64, 128])
    g2_in = din("g2", [L, 128, 256])
    dnp_in = din("dnp", [L, 4, 128, 20])
    dnp2_in = din("dnp2", [L, 4, 64, 4])
    srw_in = din("srw", [L, 4, 128, 64])
    sdn_in = din("sdn", [L, 4, 128, 64])
    mstr_c = din("mstr_c", [64, 512])
    minc_c = din("minc_c", [64, 512])
    id8_c = din("id8_c", [64, 512])
    onesb_c = din("onesb_c", [128, 128])
    onehot_c = din("onehot_c", [128, 1])
    st_out = dout("st", [L, 2, 2, 4, 128, 64])
    yT_out = dout("yT", [D, NT])
    dbgY = dout("dbgY", [D, NT]) if DEBUG else None
    dbgP = dout("dbgP", [NPC * 128, NT]) if DEBUG else None
    ptest = din("ptest", [NPC * 128, NT]) if SCAN_TEST else None
    kvT_out = dout("kvT", [L, 256, 512])

    Xs = nc.dram_tensor("Xs", [D, NT], F32).ap()
    Ps = nc.dram_tensor("Ps", [NPC * 128, NT], F32).ap()
    Ys = nc.dram_tensor("Ys", [D, NT], F32).ap()

    with ExitStack() as es:
        fw = FW(nc, es)
        psb = [es.enter_context(nc.psum_tensor("psb%d" % i, [128, 512], F32)) for i in range(8)]
        PS = Ring(psb, 'ps')
        uid = [0]

        def mk_sb(stack):
            def sb(name, shape, dt=F32):
                uid[0] += 1
                return stack.enter_context(nc.sbuf_tensor("%s_%d" % (name, uid[0]), shape, dt))
            return sb
        sb = mk_sb(es)

        ones = sb("ones", [128, 128])
        fw.dma('sp', ones[:], ones_c[:, :], [], ['ones'])
        ident = sb("ident", [128, 128])
        fw.dma('sp', ident[:], ident_c[:, :], [], ['ident'])
        onesb = sb("onesb", [128, 128])
        fw.dma('sp', onesb[:], onesb_c[:, :], [], ['onesb'])
        onehot = sb("onehot", [128, 1])
        fw.dma('sp', onehot[:], onehot_c[:, :], [], ['onehot'])
        cond = sb("cond", [128, 8, 2])
        fw.dma('sp', cond[:], condT[:, :, :], [], ['cond'])
        scond = sb("scond", [128, 8, 2], BF16)
        fw.op('act', ['cond'], ['scond'], lambda: nc.scalar.activation(scond[:], cond[:], AF.Silu))
        fnw = sb("fnw", [128, 8])
        fw.dma('sp', fnw[:], fnorm_wT[:, :], [], ['fnw'])
        modT = sb("modT", [128, 72, 2])
        nwt = sb("nwt", [128, 24])
        modA = sb("modA", [128, 24, 2])
        adab = sb("adab", [128, 72])

        def cj_of(tile):
            return 0 if tile == 0 else 1

        def trunk_phase(l, first):
            with ExitStack() as ts:
                tsb = mk_sb(ts)
                xt = tsb("xt", [128, 8, TT])
                sq = tsb("sq", [128, 8, TT])
                rstd = tsb("rstd", [128, TT])
                hT = tsb("hT", [128, 8, TT], BF16)
                actT = tsb("actT", [128, 22, TT], BF16)
                sg = Ring([tsb("sg%d" % i, [128, TT]) for i in range(2)], 'sg')
                yt = tsb("yt", [128, 8, TT], BF16)
                pt = Ring([tsb("pt%d" % i, [128, TT]) for i in range(3)], 'pt')
                wA = Ring([tsb("wA%d" % i, [128, 8, 512], BF16) for i in range(4)], 'wA')
                wB = Ring([tsb("wB%d" % i, [128, 8, 512], BF16) for i in range(4)], 'wB')
                wO = Ring([tsb("wO%d" % i, [128, 4, 1024], BF16) for i in range(4)], 'wO')

                def adaln():
                    fw.dma('sp', adab[:], ada_bT[l], [], ['adab'])
                    fw.dma('sp', nwt[:], norm_wT[l], [], ['nwt'])
                    ps, pk = PS.next()
                    src = ada_w[l].rearrange("(k p) f -> p k f", p=128)
                    for blk in range(18):
                        wt, wk = wA.next()
                        fw.dma('pool', wt[:], src[:, :, blk * 512:(blk + 1) * 512], [], [wk])
                        for mm in range(4):
                            m = blk * 4 + mm
                            for k in range(8):
                                fw.op('pe', [wk, 'scond'], [pk], lambda: nc.tensor.matmul(
                                    ps[:, 2 * m:2 * m + 2], wt[:, k, mm * 128:(mm + 1) * 128], scond[:, k, :],
                                    start=(k == 0), stop=(k == 7)), inc=(k == 7))
                    fw.op('dve', [pk, 'adab'], ['modT'], lambda: nc.vector.tensor_tensor(
                        modT[:], ps[:, 0:144].rearrange("p (m j) -> p m j", j=2),
                        adab[:].unsqueeze(2).to_broadcast([128, 72, 2]), ALU.add))
                    for n in range(3):
                        sc = modT[:, (3 * n + 1) * 8:(3 * n + 2) * 8, :]
                        fw.op('dve', ['modT', 'nwt'], ['modA'], lambda: nc.vector.scalar_tensor_tensor(
                            modA[:, n * 8:(n + 1) * 8, :], sc, 1.0,
                            nwt[:, n * 8:(n + 1) * 8].unsqueeze(2).to_broadcast([128, 8, 2]), ALU.add, ALU.mult))

                def load_x(from_input, tile):
                    src = (xT_in if from_input else Xs)
                    fw.dma('sp', xt[:], src.rearrange("(k p) t -> p k t", p=128)[:, :, tile * TT:(tile + 1) * TT],
                           [('X', tile)], ['xt'])

                def store_x(tile):
                    fw.dma('sp', Xs.rearrange("(k p) t -> p k t", p=128)[:, :, tile * TT:(tile + 1) * TT], xt[:],
                           ['xt'], [('X', tile)])

                def rms_rstd():
                    fw.op('act', ['xt'], ['sq'], lambda: nc.scalar.activation(sq[:], xt[:], AF.Square))
                    ps, pk = PS.next()
                    for k in range(8):
                        fw.op('pe', ['sq', 'ones'], [pk], lambda: nc.tensor.matmul(
                            ps[:, :], ones[:], sq[:, k, :], start=(k == 0), stop=(k == 7)), inc=(k == 7))
                    fw.op('act', [pk], ['rstd'], lambda: nc.scalar.activation(
                        rstd[:], ps[:, :], AF.Sqrt, bias=EPS, scale=1.0 / D))
                    fw.op('dve', ['rstd'], ['rstd'], lambda: nc.vector.reciprocal(rstd[:], rstd[:]))

                def modulate(n, cj):
                    rms_rstd()
                    fw.op('dve', ['xt', 'rstd'], ['sq'], lambda: nc.vector.tensor_tensor(
                        sq[:], xt[:], rstd[:].unsqueeze(1).to_broadcast([128, 8, TT]), ALU.mult))
                    for k in range(8):
                        e = 'dve' if k % 2 == 0 else 'pool'
                        eng = nc.vector if k % 2 == 0 else nc.gpsimd
                        fw.op(e, ['sq', 'modA', 'modT'], ['hT'], lambda: eng.tensor_scalar(
                            hT[:, k, :], sq[:, k, :], modA[:, n * 8 + k, cj:cj + 1],
                            modT[:, (3 * n) * 8 + k, cj:cj + 1], ALU.mult, ALU.add))

                def ffn(which, n, gidx, cj, gscale):
                    modulate(n, cj)
                    wi = ffn_w_in[l, which].rearrange("(k p) f -> p k f", p=128)
                    for g in range(6):
                        nch = 4 if g < 5 else 2
                        wa, wak = wA.next()
                        wb, wbk = wB.next()
                        fw.dma('pool', wa[:, :, 0:nch * 128], wi[:, :, g * 512:g * 512 + nch * 128], [], [wak])
                        fw.dma('pool', wb[:, :, 0:nch * 128], wi[:, :, DFF + g * 512:DFF + g * 512 + nch * 128],
                               [], [wbk])
                        for c in range(nch):
                            f = g * 4 + c
                            pg, pgk = PS.next()
                            pu, puk = PS.next()
                            for k in range(8):
                                fw.op('pe', [wak, 'hT'], [pgk], lambda: nc.tensor.matmul(
                                    pg[:, :], wa[:, k, c * 128:(c + 1) * 128], hT[:, k, :],
                                    start=(k == 0), stop=(k == 7)), inc=(k == 7))
                            for k in range(8):
                                fw.op('pe', [wbk, 'hT'], [puk], lambda: nc.tensor.matmul(
                                    pu[:, :], wb[:, k, c * 128:(c + 1) * 128], hT[:, k, :],
                                    start=(k == 0), stop=(k == 7)), inc=(k == 7))
                            s, sk = sg.next()
                            fw.op('act', [pgk], [sk], lambda: nc.scalar.activation(s[:], pg[:, :], AF.Silu))
                            fw.op('dve', [sk, puk], [('actT', f)], lambda: nc.vector.tensor_tensor(
                                actT[:, f, :], s[:], pu[:, :], ALU.mult))
                    wo = ffn_w_out[l, which].rearrange("(f p) d -> p f d", p=128)
                    for g in range(6):
                        nch = 4 if g < 5 else 2
                        wt, wk = wO.next()
                        fw.dma('pool', wt[:, 0:nch, :], wo[:, g * 4:g * 4 + nch, :], [], [wk])
                        for c in range(nch):
                            f = g * 4 + c
                            for j in range(8):
                                fw.op('pe', [wk, ('actT', f)], ['ps%d' % j], lambda: nc.tensor.matmul(
                                    psb[j][:, :], wt[:, c, j * 128:(j + 1) * 128], actT[:, f, :],
                                    start=(f == 0), stop=(f == 21)))
                    for j in range(8):
                        gcol = modT[:, gidx * 8 + j, cj:cj + 1]
                        t, tk = pt.next()
                        fw.op('act', ['ps%d' % j, 'modT'], [tk], lambda: nc.scalar.activation(
                            t[:], psb[j][:, :], AF.Copy, scale=gcol))
                        fw.op('dve', [tk, 'xt'], ['xt'], lambda: nc.vector.scalar_tensor_tensor(
                            xt[:, j, :], t[:], gscale, xt[:, j, :], ALU.mult, ALU.add))

                def proj(tile, cj):
                    modulate(1, cj)
                    wi = w_in[l].rearrange("(k p) f -> p k f", p=128)
                    for g in range(6):
                        ncol = 512 if g < 5 else PROJ_W - 2560
                        wa, wak = wA.next()
                        fw.dma('pool', wa[:, :, 0:ncol], wi[:, :, g * 512:g * 512 + ncol], [], [wak])
                        for c in range((ncol + 127) // 128):
                            m = min(128, ncol - c * 128)
                            f = g * 4 + c
                            ps, pk = PS.next()
                            for k in range(8):
                                fw.op('pe', [wak, 'hT'], [pk], lambda: nc.tensor.matmul(
                                    ps[0:m, :], wa[:, k, c * 128:c * 128 + m], hT[:, k, :],
                                    start=(k == 0), stop=(k == 7)), inc=(k == 7))
                            t, tk = pt.next()
                            fw.op('act', [pk], [tk], lambda: nc.scalar.copy(t[0:m, :], ps[0:m, :]))
                            fw.dma('sp', Ps[f * 128:f * 128 + m, tile * TT:(tile + 1) * TT], t[0:m, :],
                                   [tk], [('P', tile, f)])
                            if tile == 0 and f in (12, 13):
                                r0 = 0 if f == 12 else 128
                                fw.dma('sp', kvT_out[l, r0:r0 + 128, :], t[:, :], [tk], ['kvout'])

                def mix_out(tile, cj):
                    fw.dma('pool', yt[:], Ys.rearrange("(k p) t -> p k t", p=128)[:, :, tile * TT:(tile + 1) * TT],
                           [('Y', tile)], ['yt'])
                    wo = w_out[l].rearrange("(f p) d -> p f d", p=128)
                    for g in range(2):
                        wt, wk = wO.next()
                        fw.dma('pool', wt[:, :, :], wo[:, g * 4:g * 4 + 4, :], [], [wk])
                        for c in range(4):
                            f = g * 4 + c
                            for j in range(8):
                                fw.op('pe', [wk, 'yt'], ['ps%d' % j], lambda: nc.tensor.matmul(
                                    psb[j][:, :], wt[:, c, j * 128:(j + 1) * 128], yt[:, f, :],
                                    start=(f == 0), stop=(f == 7)))
                    for j in range(8):
                        gcol = modT[:, 5 * 8 + j, cj:cj + 1]
                        fw.op('dve', ['ps%d' % j, 'modT', 'xt'], ['xt'], lambda: nc.vector.scalar_tensor_tensor(
                            xt[:, j, :], psb[j][:, :], gcol, xt[:, j, :], ALU.mult, ALU.add))

                if first:
                    adaln()
                    for tile in range(NTILE):
                        cj = cj_of(tile)
                        load_x(l == 0, tile)
                        ffn(0, 0, 2, cj, 0.5)
                        store_x(tile)
                        proj(tile, cj)
                else:
                    for tile in range(NTILE):
                        cj = cj_of(tile)
                        load_x(False, tile)
                        mix_out(tile, cj)
                        ffn(1, 2, 8, cj, 0.5)
                        if l < L - 1:
                            store_x(tile)
                        else:
                            rms_rstd()
                            fw.op('dve', ['xt', 'rstd'], ['sq'], lambda: nc.vector.tensor_tensor(
                                sq[:], xt[:], rstd[:].unsqueeze(1).to_broadcast([128, 8, TT]), ALU.mult))
                            fw.op('dve', ['sq', 'fnw'], ['sq'], lambda: nc.vector.tensor_tensor(
                                sq[:], sq[:], fnw[:].unsqueeze(2).to_broadcast([128, 8, TT]), ALU.mult))
                            fw.dma('sp', yT_out.rearrange("(k p) t -> p k t", p=128)[:, :, tile * TT:(tile + 1) * TT],
                                   sq[:], ['sq'], ['yout'])
                fw.barrier()

        def zero_rows(msb, r0, r1):
            z = msb("z", [128, 512])
            fw.op('pool', [], ['z'], lambda: nc.gpsimd.memset(z[:], 0.0))
            for tile in range(NTILE):
                for k in range(r0 // 128, r1 // 128):
                    fw.dma('sp', Ys[k * 128:(k + 1) * 128, tile * TT:(tile + 1) * TT], z[:], ['z'], [('Y', tile)])

        def attention_phase(l):
            with ExitStack() as ms:
                msb = mk_sb(ms)
                allP = [('P', t) for t in range(NTILE)]
                allY = [('Y', t) for t in range(NTILE)]
                cosT = msb("cosT", [64, 2048]); sinT = msb("sinT", [64, 2048])
                rotT = msb("rotT", [64, 64]); amask = msb("amask", [128, 3, 384]); sink = msb("sink", [128, 8])
                fw.dma('sp', cosT[:], cos_c[:, :], [], ['cosT'])
                fw.dma('sp', sinT[:], sin_c[:, :], [], ['sinT'])
                fw.dma('sp', rotT[:], rotT_c[:, :], [], ['rotT'])
                fw.dma('sp', amask[:], amask_c[:, :, :], [], ['amask'])
                fw.dma('sp', sink[:], sinkb[l], [], ['sink'])
                kpad = msb("kpad", [64, 2304]); kraw = msb("kraw", [64, 2048]); vraw = msb("vraw", [64, 2048])
                qraw = msb("qraw", [64, 2048]); qr = msb("qr", [64, 2048]); tmp = msb("tmp", [64, 512])
                vtok = msb("vtok", [128, 16, 64]); kctxT = msb("kctxT", [64, 512])
                cktok = msb("cktok", [128, 4, 64]); vctx = msb("vctx", [128, 4, 64])
                ybuf = msb("ybuf", [64, 2048])
                scr = Ring([msb("sc%d" % i, [128, 904]) for i in range(3)], 'sc')
                prr = Ring([msb("pr%d" % i, [128, 904]) for i in range(3)], 'pr')
                pTr = Ring([msb("pT%d" % i, [128, 7, 128]) for i in range(3)], 'pT')
                st = Ring([msb("st%d" % i, [128, 4]) for i in range(3)], 'st')
                fw.op('pool', [], ['kpad'], lambda: nc.gpsimd.memset(kpad[:], 0.0))

                def rope(src, skey, dst_ap, dkey, T):
                    for c in range(T // 512):
                        sl = slice(c * 512, (c + 1) * 512)
                        ps, pk = PS.next()
                        fw.op('pe', [skey, 'rotT'], [pk], lambda: nc.tensor.matmul(
                            ps[0:64, :], rotT[:], src[:, sl], start=True, stop=True))
                        fw.op('dve', [pk, 'sinT'], ['tmp'], lambda: nc.vector.tensor_tensor(
                            tmp[:], ps[0:64, :], sinT[:, sl], ALU.mult))
                        fw.op('pool', [skey, 'cosT'], [dkey], lambda: nc.gpsimd.tensor_tensor(
                            dst_ap[:, sl], src[:, sl], cosT[:, sl], ALU.mult))
                        fw.op('dve', ['tmp', dkey], [dkey], lambda: nc.vector.tensor_tensor(
                            dst_ap[:, sl], dst_ap[:, sl], tmp[:], ALU.add))

                def transposes_to(src_fn, nblk, dst_fn, rkeys, wkey, pin, pout):
                    for b0 in range(0, nblk, 4):
                        nb = min(4, nblk - b0)
                        ps, pk = PS.next()
                        for b in range(nb):
                            fw.op('pe', rkeys + ['ident'], [pk], lambda: nc.tensor.transpose(
                                ps[0:pout, b * pin:(b + 1) * pin], src_fn(b0 + b), ident[0:pin, 0:pin]))
                        fw.op('act', [pk], [wkey], lambda: nc.scalar.copy(
                            dst_fn(b0, nb), ps[0:pout, 0:nb * pin].rearrange("p (b i) -> p b i", i=pin)))

                def softmax_pv(qlhsT, qkey, parts, hq, vblocks, out_ap, okey):
                    sc, sk = scr.next()
                    s4, s4k = st.next()
                    pr, prk = prr.next()
                    pT, pTk = pTr.next()
                    off = 0
                    for (rhs, rkey, ncol, mask) in parts:
                        ps, pk = PS.next()
                        fw.op('pe', [qkey, rkey], [pk], lambda: nc.tensor.matmul(
                            ps[:, 0:ncol], qlhsT, rhs, start=True, stop=True))
                        yield
                        if mask is not None:
                            fw.op('dve', [pk, 'amask'], [sk], lambda: nc.vector.scalar_tensor_tensor(
                                sc[:, off:off + ncol], ps[:, 0:ncol], 0.125, mask, ALU.mult, ALU.add))
                        else:
                            fw.op('act', [pk], [sk], lambda: nc.scalar.mul(sc[:, off:off + ncol], ps[:, 0:ncol], 0.125))
                        off += ncol
                    ntot = off
                    fw.op('pool', ['sink'], [sk], lambda: nc.gpsimd.tensor_copy(sc[:, ntot:ntot + 1], sink[:, hq:hq + 1]))
                    yield
                    fw.op('dve', [sk], [s4k], lambda: nc.vector.reduce_max(
                        s4[:, 0:1], sc[:, 0:ntot + 1], mybir.AxisListType.X))
                    yield
                    fw.op('dve', [s4k], [s4k], lambda: nc.vector.tensor_scalar(
                        s4[:, 1:2], s4[:, 0:1], -1.0, None, ALU.mult))
                    yield
                    fw.op('act', [sk, s4k], [prk], lambda: nc.scalar.activation(
                        pr[:, 0:ntot + 1], sc[:, 0:ntot + 1], AF.Exp, bias=s4[:, 1:2], scale=1.0))
                    yield
                    fw.op('dve', [prk], [s4k], lambda: nc.vector.reduce_sum(
                        s4[:, 2:3], pr[:, 0:ntot + 1], mybir.AxisListType.X))
                    yield
                    fw.op('dve', [s4k], [s4k], lambda: nc.vector.reciprocal(s4[:, 3:4], s4[:, 2:3]))
                    yield
                    fw.op('dve', [prk, s4k], [prk], lambda: nc.vector.tensor_scalar(
                        pr[:, 0:ntot], pr[:, 0:ntot], s4[:, 3:4], None, ALU.mult))
                    yield
                    nblk = ntot // 128
                    transposes_to(lambda b: pr[:, b * 128:(b + 1) * 128], nblk,
                                  lambda b0, nb: pT[:, b0:b0 + nb, :], [prk], pTk, 128, 128)
                    yield
                    po, pok = PS.next()
                    for i, (b, vap, vk) in enumerate(vblocks):
                        fw.op('pe', [pTk, vk], [pok], lambda: nc.tensor.matmul(
                            po[0:64, 0:128], vap, pT[:, b, :], start=(i == 0), stop=(i == len(vblocks) - 1)))
                    yield
                    fw.op('act', [pok], [okey], lambda: nc.scalar.copy(out_ap, po[0:64, 0:128]))

                def run_interleaved(gens, width=2):
                    live = []
                    gens = list(gens)
                    while gens or live:
                        while gens and len(live) < width:
                            live.append(gens.pop(0))
                        nxt = []
                        for g_ in live:
                            try:
                                next(g_)
                                nxt.append(g_)
                            except StopIteration:
                                pass
                        live = nxt

                for s in range(2):
                    c0 = s * 256
                    for g in range(2):
                        fw.dma('sp', kraw[:, 0:256], Ps[1536 + g * 64:1600 + g * 64, c0:c0 + 256], [('P', 0)], ['kraw'])
                        fw.dma('sp', vraw[:, 0:256], Ps[1664 + g * 64:1728 + g * 64, c0:c0 + 256], [('P', 0)], ['vraw'])
                        transposes_to(lambda b: vraw[:, b * 128:(b + 1) * 128], 2,
                                      lambda b0, nb: vtok[:, b0:b0 + nb, :], ['vraw'], 'vtok', 64, 128)
                        for gi in range(4):
                            hq = g * 4 + gi
                            fw.dma('sp', qraw[:, 0:256], Ps[1024 + hq * 64:1088 + hq * 64, c0:c0 + 256],
                                   [('P', 0)], ['qraw'])
                            run_interleaved([softmax_pv(qraw[:, n * 128:(n + 1) * 128], 'qraw',
                                                        [(kraw[:, 0:256], 'kraw', 256, None)], hq,
                                                        [(b, vtok[:, b, :], 'vtok') for b in range(2)],
                                                        ybuf[:, n * 128:(n + 1) * 128], 'ybuf') for n in range(2)])
                            fw.dma('sp', Ys[256 + hq * 64:320 + hq * 64, c0:c0 + 256], ybuf[:, 0:256],
                                   ['ybuf'], [('Y', 0)])
                sP = allP[1:]
                for g in range(2):
                    fw.dma('sp', kraw[:], Ps[1536 + g * 64:1600 + g * 64, 512:NT], sP, ['kraw'])
                    rope(kraw, 'kraw', kpad[:, 128:2176], 'kpad', 2048)
                    fw.dma('sp', vraw[:], Ps[1664 + g * 64:1728 + g * 64, 512:NT], sP, ['vraw'])
                    transposes_to(lambda b: vraw[:, b * 128:(b + 1) * 128], 16,
                                  lambda b0, nb: vtok[:, b0:b0 + nb, :], ['vraw'], 'vtok', 64, 128)
                    fw.dma('sp', cktok[:], ck_in[l].rearrange("(b p) f -> p b f", p=128)[:, :, g * 64:(g + 1) * 64],
                           [], ['cktok'])
                    fw.dma('sp', vctx[:], cv_in[l].rearrange("(b p) f -> p b f", p=128)[:, :, g * 64:(g + 1) * 64],
                           [], ['vctx'])
                    transposes_to(lambda b: cktok[:, b, :], 4,
                                  lambda b0, nb: kctxT[:, b0 * 128:(b0 + nb) * 128].rearrange("p (b i) -> p b i", i=128),
                                  ['cktok'], 'kctxT', 128, 64)
                    for gi in range(4):
                        hq = g * 4 + gi
                        fw.dma('sp', qraw[:], Ps[1024 + hq * 64:1088 + hq * 64, 512:NT], sP, ['qraw'])
                        rope(qraw, 'qraw', qr, 'qr', 2048)
                        def items():
                            for n in range(16):
                                var = 0 if n == 0 else (2 if n == 15 else 1)
                                vbl = [(b, vtok[:, n - 1 + b, :], 'vtok') for b in range(3) if 0 <= n - 1 + b < 16]
                                vbl += [(3 + b, vctx[:, b, :], 'vctx') for b in range(4)]
                                yield softmax_pv(qr[:, n * 128:(n + 1) * 128], 'qr',
                                                 [(kpad[:, n * 128:n * 128 + 384], 'kpad', 384, amask[:, var, :]),
                                                  (kctxT[:, :], 'kctxT', 512, None)], hq, vbl,
                                                 ybuf[:, n * 128:(n + 1) * 128], 'ybuf')
                        run_interleaved(items(), 3)
                        fw.dma('sp', Ys[256 + hq * 64:320 + hq * 64, 512:NT], ybuf[:], ['ybuf'], allY[1:])
                fw.barrier()

        def scan_phase(l, which):
            with ExitStack() as ms:
                msb = mk_sb(ms)
                TM = 2048
                big = lambda name: msb(name, [128, TM])
                R_, K_, V_, A_, B_ = big("R"), big("K"), big("V"), big("A"), big("B")
                CS, CSX = big("CS"), big("CSX")
                t1, t2 = big("t1"), big("t2")
                padA, padB = msb("padA", [128, TM + 4]), msb("padB", [128, TM + 4])
                X1, X2 = big("X1"), big("X2")
                osc = msb("osc", [64, 2, TM])
                GT = msb("GT", [64, TM]); BON = msb("BON", [64, TM]) if which == 'rwkv' else None
                csrow = msb("csrow", [64, 2, TM]) if which == 'dn' else None
                prm = msb("prm", [128, 32]); prm2 = msb("prm2", [64, 4])
                w2c = msb("w2c", [128, 128]); a2c = msb("a2c", [128, 128]); g2h = msb("g2h", [128, 64])
                S = msb("S", [128, 64]); wendc = msb("wendc", [128, 4]); cendc = msb("cendc", [128, 4])
                negA = msb("negA", [128, 2]); negA2 = msb("negA2", [64, 2])
                mstr = msb("mstr", [64, 512]); minc = msb("minc", [64, 512]); id8 = msb("id8", [64, 512])
                fw.dma('sp', mstr[:], mstr_c[:, :], [], ['mstr'])
                fw.dma('sp', minc[:], minc_c[:, :], [], ['minc'])
                fw.dma('sp', id8[:], id8_c[:, :], [], ['id8'])
                bt = {}
                for nm in ("Wi", "Wx", "Winv", "Wrem", "oR", "oA", "oB", "oK", "oBh", "oKh"):
                    bt[nm] = msb("b" + nm, [128, 256])
                cc = {}
                for nm in ("N", "NT", "Lak", "Mrb", "Mrk", "X", "XT", "Pm", "DTs", "DTi", "LakV", "U0", "U"):
                    cc[nm] = msb("c" + nm, [64, 512])
                tk = {}
                for nm in ("At", "Bh", "Kh", "V"):
                    tk[nm] = msb("k" + nm, [64, 4, 128])
                AhT = msb("AhT", [128, 4, 64])
                ncol = msb("ncol", [64, 8])
                fw.op('pool', [], ['padA'], lambda: nc.gpsimd.memset(padA[:], 0.0))
                fw.op('pool', [], ['padB'], lambda: nc.gpsimd.memset(padB[:], 0.0))

                def pieces(T):
                    return [(p * 512, min(512, T - p * 512)) for p in range((T + 511) // 512)]

                def zpad(pad, padk, T, off):
                    fw.op('pool', [], [padk], lambda: nc.gpsimd.memset(pad[:, 0:off], 0.0))
                    fw.op('pool', [], [padk], lambda: nc.gpsimd.memset(pad[:, off + T:off + T + off], 0.0))

                def load_dup(pad, padk, row0, c0, T, off):
                    zpad(pad, padk, T, off)
                    for d in range(2):
                        fw.dma('sp', pad[d * 64:(d + 1) * 64, off:off + T], Ps[row0:row0 + 64, c0:c0 + T],
                               allP, [padk])

                def sum_halves(src, skey, T, dst, dkey, mat, func, bias, scale):
                    for (o, n) in pieces(T):
                        ps, pk = PS.next()
                        fw.op('pe', [skey, 'ones', 'onesb'], [pk], lambda: nc.tensor.matmul(
                            ps[:, 0:n], mat, src[:, o:o + n], start=True, stop=True))
                        fw.op('act', [pk], [dkey], lambda: nc.scalar.activation(
                            dst[:, o:o + n], ps[:, 0:n], func, bias=bias, scale=scale))

                def engine(T, mode, s0_ap, fin_ap):
                    nb = T // 256
                    stage = SCAN_TEST[1] if SCAN_TEST else 9
                    if stage < 2:
                        fw.op('pool', [], ['osc'], lambda: nc.gpsimd.memset(osc[:], 0.0))
                        return
                    if s0_ap is None:
                        fw.op('pool', [], ['S'], lambda: nc.gpsimd.memset(S[:], 0.0))
                    else:
                        fw.dma('sp', S[:], s0_ap, [], ['S'])
                    for b in range(nb):
                        sl = slice(b * 256, (b + 1) * 256)
                        v3 = lambda t: t[:, sl].rearrange("p (c i) -> p c i", i=64)
                        fw.op('dve', ['t1'], ['CS'], lambda: nc.vector.tensor_tensor_scan(
                            CS[:, sl], t1[:, sl], t1[:, sl], 0.0, ALU.add, ALU.bypass))
                        fw.op('dve', ['CS', 't1'], ['CSX'], lambda: nc.vector.tensor_tensor(
                            CSX[:, sl], CS[:, sl], t1[:, sl], ALU.subtract))
                        fw.op('dve', ['CSX'], ['cendc'], lambda: nc.vector.tensor_copy(
                            cendc[:, :].unsqueeze(2), v3(CSX)[:, :, 0:1]))
                        fw.op('dve', ['CS', 'cendc'], ['CS'], lambda: nc.vector.tensor_tensor(
                            v3(CS), v3(CS), cendc[:, :].unsqueeze(2).to_broadcast([128, 4, 64]), ALU.subtract))
                        fw.op('dve', ['CS', 't1'], ['CSX'], lambda: nc.vector.tensor_tensor(
                            CSX[:, sl], CS[:, sl], t1[:, sl], ALU.subtract))
                        fw.op('dve', ['CS'], ['cendc'], lambda: nc.vector.tensor_copy(
                            cendc[:, :].unsqueeze(2), v3(CS)[:, :, 63:64]))
                        fw.op('act', ['cendc'], ['wendc'], lambda: nc.scalar.activation(wendc[:], cendc[:], AF.Exp))
                        Wi, Wx, Winv, Wrem = bt["Wi"], bt["Wx"], bt["Winv"], bt["Wrem"]
                        fw.op('act', ['CS'], ['Wi'], lambda: nc.scalar.activation(Wi[:], CS[:, sl], AF.Exp))
                        fw.op('dve', ['CS', 'cendc'], ['Wrem'], lambda: nc.vector.tensor_tensor(
                            Wrem[:].rearrange("p (c i) -> p c i", i=64),
                            cendc[:, :].unsqueeze(2).to_broadcast([128, 4, 64]), v3(CS), ALU.subtract))
                        fw.op('act', ['Wrem'], ['Wrem'], lambda: nc.scalar.activation(Wrem[:], Wrem[:], AF.Exp))
                        if SCAN_SUB <= 10:
                            continue
                        mul = lambda e, out, okey, a_, akey, b_, bkey: fw.op(
                            e, [akey, bkey], [okey],
                            lambda: (nc.vector if e == 'dve' else nc.gpsimd).tensor_tensor(out, a_, b_, ALU.mult))
                        mul('dve', bt["oBh"][:], 'oBh', B_[:, sl], 'B', Wrem[:], 'Wrem')
                        mul('pool', bt["oKh"][:], 'oKh', K_[:, sl], 'K', Wrem[:], 'Wrem')
                        mul('dve', bt["oR"][:], 'oR', R_[:, sl], 'R', Wi[:], 'Wi')
                        if mode == 'vec':
                            fw.op('act', ['CSX'], ['Wx'], lambda: nc.scalar.activation(Wx[:], CSX[:, sl], AF.Exp))
                            fw.op('act', ['CS'], ['Winv'], lambda: nc.scalar.activation(
                                Winv[:], CS[:, sl], AF.Exp, scale=-1.0))
                            mul('pool', bt["oA"][:], 'oA', A_[:, sl], 'A', Wx[:], 'Wx')
                            mul('dve', bt["oB"][:], 'oB', B_[:, sl], 'B', Winv[:], 'Winv')
                            mul('pool', bt["oK"][:], 'oK', K_[:, sl], 'K', Winv[:], 'Winv')
                            Rop, Aop, Bop, Kop = (bt["oR"], 'oR', 0), (bt["oA"], 'oA', 0), (bt["oB"], 'oB', 0), (bt["oK"], 'oK', 0)
                            DTs, DTsk, DTi, DTik = mstr, 'mstr', minc, 'minc'
                        else:
                            mul('pool', bt["oA"][:], 'oA', A_[:, sl], 'A', Wi[:], 'Wi')
                            o0 = b * 256
                            Rop, Aop, Bop, Kop = (R_, 'R', o0), (A_, 'A', o0), (B_, 'B', o0), (K_, 'K', o0)
                            ps, pk = PS.next()
                            for c in range(4):
                                for d in range(2):
                                    fw.op('pe', ['csrow', 'onehot'], [pk], lambda: nc.tensor.matmul(
                                        ps[0:64, c * 2 + d:c * 2 + d + 1],
                                        csrow[:, d, o0 + c * 64:o0 + (c + 1) * 64], onehot[0:64, 0:1],
                                        start=True, stop=True))
                            fw.op('act', [pk], ['ncol'], lambda: nc.scalar.mul(ncol[:], ps[0:64, 0:8], -1.0))
                            fw.op('dve', ['csrow', 'ncol'], ['DTs'], lambda: nc.vector.tensor_tensor(
                                cc["DTs"][:].rearrange("p (c d i) -> p c d i", d=2, i=64),
                                csrow[:, :, o0:o0 + 256].rearrange("p d (c i) -> p c d i", i=64),
                                ncol[:].rearrange("p (c d) -> p c d", d=2).unsqueeze(3).to_broadcast([64, 4, 2, 64]),
                                ALU.add))
                            fw.op('dve', ['DTs'], ['DTs'], lambda: nc.vector.tensor_scalar(
                                cc["DTs"][:], cc["DTs"][:], 0.0, None, ALU.min))
                            fw.op('act', ['DTs'], ['DTs'], lambda: nc.scalar.activation(cc["DTs"][:], cc["DTs"][:], AF.Exp))
                            fw.op('pool', ['DTs', 'minc'], ['DTi'], lambda: nc.gpsimd.tensor_tensor(
                                cc["DTi"][:], cc["DTs"][:], minc[:], ALU.mult))
                            fw.op('dve', ['DTs', 'mstr'], ['DTs'], lambda: nc.vector.tensor_tensor(
                                cc["DTs"][:], cc["DTs"][:], mstr[:], ALU.mult))
                            DTs, DTsk, DTi, DTik = cc["DTs"], 'DTs', cc["DTi"], 'DTi'
                        At, Atk, Rt, Rtk = bt["oA"], 'oA', bt["oR"], 'oR'
                        if SCAN_SUB <= 11:
                            continue

                        def cxc(L_, R2, mask, mkey, dst, dkey):
                            ps, pk = PS.next()
                            (lt, lk, lo), (rt, rk, ro) = L_, R2
                            for d in range(2):
                                for c in range(4):
                                    fw.op('pe', [lk, rk], [pk], lambda: nc.tensor.matmul(
                                        ps[0:64, (c * 2 + d) * 64:(c * 2 + d + 1) * 64],
                                        lt[d * 64:(d + 1) * 64, lo + c * 64:lo + (c + 1) * 64],
                                        rt[d * 64:(d + 1) * 64, ro + c * 64:ro + (c + 1) * 64], start=True, stop=True), rt=d)
                            fw.op('dve', [pk, mkey], [dkey], lambda: nc.vector.tensor_tensor(
                                dst[:], ps[0:64, :], mask[:], ALU.mult))
                        cxc(Bop, Aop, DTs, DTsk, cc["N"], 'N')
                        same_bk = (mode == 'scal')
                        cLak = (lambda: fw.op('pool', ['N'], ['Lak'], lambda: nc.gpsimd.tensor_copy(cc["Lak"][:], cc["N"][:]))) \
                            if same_bk else (lambda: cxc(Kop, Aop, DTs, DTsk, cc["Lak"], 'Lak'))

                        def mm8(lhs, lkey, rhs, rkey, evac):
                            ps, pk = PS.next()
                            for m in range(8):
                                fw.op('pe', [lkey, rkey], [pk], lambda: nc.tensor.matmul(
                                    ps[0:64, m * 64:(m + 1) * 64], lhs(m), rhs(m), start=True, stop=True), inc=(m == 7))
                            evac(ps, pk)
                        blk = lambda t: (lambda m: t[:, m * 64:(m + 1) * 64])
                        cp = lambda dst, dkey: (lambda ps, pk: fw.op('act', [pk], [dkey], lambda: nc.scalar.copy(dst[:], ps[0:64, :])))
                        ps, pk = PS.next()
                        for m in range(8):
                            fw.op('pe', ['N', 'ident'], [pk], lambda: nc.tensor.transpose(
                                ps[0:64, m * 64:(m + 1) * 64], cc["N"][:, m * 64:(m + 1) * 64], ident[0:64, 0:64]))
                        cp(cc["NT"], 'NT')(ps, pk)
                        fw.op('dve', ['N', 'id8'], ['Pm'], lambda: nc.vector.tensor_tensor(cc["Pm"][:], cc["N"][:], id8[:], ALU.add))
                        cLak()

                        def tok_major(nm, src, skey, so):
                            ps, pk = PS.next()
                            for c in range(4):
                                fw.op('pe', [skey, 'ident'], [pk], lambda: nc.tensor.transpose(
                                    ps[0:64, c * 128:(c + 1) * 128], src[:, so + c * 64:so + (c + 1) * 64], ident[:, :]))
                            fw.op('dve', [pk], ['k' + nm], lambda: nc.vector.tensor_copy(
                                tk[nm][:], ps[0:64, :].rearrange("p (c f) -> p c f", f=128)))
                        fill = [lambda: None,
                                lambda: tok_major("V", V_, 'V', b * 256),
                                lambda: cxc(Bop, Rop, DTi, DTik, cc["Mrb"], 'Mrb'),
                                lambda: tok_major("At", At, Atk, 0),
                                (lambda: fw.op('pool', ['Mrb'], ['Mrk'], lambda: nc.gpsimd.tensor_copy(cc["Mrk"][:], cc["Mrb"][:])))
                                if same_bk else (lambda: cxc(Kop, Rop, DTi, DTik, cc["Mrk"], 'Mrk')),
                                lambda: tok_major("Bh", bt["oBh"], 'oBh', 0),
                                lambda: tok_major("Kh", bt["oKh"], 'oKh', 0)]
                        bufX = [(cc["N"], 'N'), (cc["X"], 'X')]
                        bufXT = [(cc["NT"], 'NT'), (cc["XT"], 'XT')]
                        padd = lambda ps, pk: fw.op('dve', [pk, 'Pm'], ['Pm'], lambda: nc.vector.tensor_tensor(
                            cc["Pm"][:], cc["Pm"][:], ps[0:64, :], ALU.add))
                        pend = None
                        for lev in range(5):
                            (Xc, Xk), (XTc, XTk) = bufX[lev % 2], bufXT[lev % 2]
                            (nX, nXk), (nXT, nXTk) = bufX[(lev + 1) % 2], bufXT[(lev + 1) % 2]
                            mm8(blk(Xc), Xk, blk(XTc), XTk, cp(nXT, nXTk))
                            if lev < 4:
                                mm8(blk(XTc), XTk, blk(Xc), Xk, cp(nX, nXk))
                            if pend is not None:
                                mm8(blk(pend[0]), pend[1], blk(cc["Pm"]), 'Pm', padd)
                            if fill:
                                fill.pop(0)()
                            if fill and lev % 2 == 0:
                                fill.pop(0)()
                            pend = (nXT, nXTk)
                        mm8(blk(pend[0]), pend[1], blk(cc["Pm"]), 'Pm', padd)
                        while fill:
                            fill.pop(0)()
                        TTm = cc["Pm"]
                        vt = lambda m: tk["V"][:, m // 2, (m % 2) * 64:(m % 2 + 1) * 64]
                        mm8(blk(cc["Lak"]), 'Lak', vt, 'kV', cp(cc["LakV"], 'LakV'))
                        mm8(blk(TTm), 'Pm', blk(cc["LakV"]), 'LakV', cp(cc["U0"], 'U0'))
                        if SCAN_SUB <= 16:
                            continue
                        ps, pk = PS.next()
                        for c in range(4):
                            fw.op('pe', ['kAt', 'Pm'], [pk], lambda: nc.tensor.matmul(
                                ps[:, c * 128:(c + 1) * 128], tk["At"][:, c, :], TTm[:, c * 128:(c + 1) * 128],
                                start=True, stop=True))
                        for d in range(2):
                            e = 'act' if d == 0 else 'dve'
                            src_ = ps[d * 64:(d + 1) * 64, :].rearrange("p (c d i) -> p c d i", d=2, i=64)[:, :, d, :]
                            if d == 0:
                                fw.op('act', [pk], ['AhT'], lambda: nc.scalar.copy(AhT[0:64, :, :], src_))
                            else:
                                fw.op('dve', [pk], ['AhT'], lambda: nc.vector.tensor_copy(AhT[64:128, :, :], src_))
                        if stage < 3:
                            fw.op('pool', [], ['osc'], lambda: nc.gpsimd.memset(osc[:], 0.0))
                        for c in range(4 if stage >= 3 else 0):
                            col = b * 256 + c * 64
                            ps, pk = PS.next()
                            for d in range(2):
                                fw.op('pe', ['AhT', 'S'], [pk], lambda: nc.tensor.matmul(
                                    ps[0:64, d * 64:(d + 1) * 64], AhT[d * 64:(d + 1) * 64, c, :],
                                    S[d * 64:(d + 1) * 64, :], start=True, stop=True), rt=d)
                            fw.op('dve', [pk, 'U0'], ['U'], lambda: nc.vector.tensor_tensor(
                                cc["U"][:, 0:128], ps[0:64, 0:128], cc["U0"][:, c * 128:(c + 1) * 128], ALU.add))
                            po, pok = PS.next()
                            for d in range(2):
                                m = c * 2 + d
                                if d == 0:
                                    fw.op('pe', ['S', Rtk], [pok], lambda: nc.tensor.matmul(
                                        po[0:64, 0:64], S[0:64, :], Rt[0:64, c * 64:(c + 1) * 64], start=True, stop=False))
                                fw.op('pe', ['U', 'Mrb'], [pok], lambda: nc.tensor.matmul(
                                    po[0:64, d * 64:(d + 1) * 64], cc["U"][:, d * 64:(d + 1) * 64],
                                    cc["Mrb"][:, m * 64:(m + 1) * 64], start=(d == 1), stop=False))
                                fw.op('pe', ['kV', 'Mrk'], [pok], lambda: nc.tensor.matmul(
                                    po[0:64, d * 64:(d + 1) * 64], vt(m),
                                    cc["Mrk"][:, m * 64:(m + 1) * 64], start=False, stop=(d == 0)))
                            fw.op('pe', ['S', Rtk], [pok], lambda: nc.tensor.matmul(
                                po[0:64, 64:128], S[64:128, :], Rt[64:128, c * 64:(c + 1) * 64], start=False, stop=True), rt=1)
                            fw.op('act', [pok], ['osc'], lambda: nc.scalar.copy(
                                osc[:, :, col:col + 64], po[0:64, 0:128].rearrange("p (d i) -> p d i", i=64)))
                            pS, pSk = PS.next()
                            fw.op('pe', ['kBh', 'U'], [pSk], lambda: nc.tensor.matmul(
                                pS[:, 0:128], tk["Bh"][:, c, :], cc["U"][:, 0:128], start=True, stop=False))
                            fw.op('pe', ['kKh', 'kV'], [pSk], lambda: nc.tensor.matmul(
                                pS[:, 0:128], tk["Kh"][:, c, :], tk["V"][:, c, :], start=False, stop=True))
                            for d in range(2):
                                fw.op('dve', [pSk, 'wendc', 'S'], ['S'], lambda: nc.vector.scalar_tensor_tensor(
                                    S[d * 64:(d + 1) * 64, :], S[d * 64:(d + 1) * 64, :], wendc[d * 64:(d + 1) * 64, c:c + 1],
                                    pS[d * 64:(d + 1) * 64, d * 64:(d + 1) * 64], ALU.mult, ALU.add))
                    if fin_ap is not None:
                        fw.dma('sp', fin_ap, S[:], ['S'], ['stout'])

                def unit_rwkv(c0, T, h, s0_ap, fin_ap, ycols):
                    fw.dma('sp', prm[:, 0:12], rwp_in[l, h], [], ['prm'])
                    fw.dma('sp', w2c[0:64, :], w2cat_in[l, h], [], ['w2c'])
                    fw.dma('sp', a2c[64:128, :], a2cat_in[l, h], [], ['a2c'])
                    fw.dma('sp', g2h[:], g2_in[l][:, h * 64:(h + 1) * 64], [], ['g2h'])
                    H0, H1 = slice(0, 64), slice(64, 128)

                    def shift(pad, padk, mucol, dst, dkey, rev):
                        x = pad[:, 1:T + 1]
                        fw.op('dve', [padk], ['t2'], lambda: nc.vector.tensor_tensor(
                            t2[:, 0:T], pad[:, 0:T], pad[:, 2:T + 2], ALU.add))
                        fw.op('dve', ['t2', padk], ['t2'], lambda: nc.vector.scalar_tensor_tensor(
                            t2[:, 0:T], t2[:, 0:T], 0.5, x, ALU.mult, ALU.subtract))
                        if not rev:
                            fw.op('dve', ['t2', padk, 'prm'], [dkey], lambda: nc.vector.scalar_tensor_tensor(
                                dst[:, 0:T], t2[:, 0:T], mucol, x, ALU.mult, ALU.add))
                        else:
                            fw.op('dve', ['t2', padk, 'prm'], [dkey], lambda: nc.vector.scalar_tensor_tensor(
                                dst[H0, 0:T], t2[H0, 0:T], mucol[H0], pad[H0, 1:T + 1], ALU.mult, ALU.add))
                            fw.op('dve', ['t2', padk, 'prm'], [dkey], lambda: nc.vector.scalar_tensor_tensor(
                                dst[H1, 0:T], t2[H1, 0:T][:, ::-1], mucol[H1], pad[H1, 1:T + 1][:, ::-1], ALU.mult, ALU.add))
                    load_dup(padA, 'padA', h * 64, c0, T, 1)
                    shift(padA, 'padA', prm[:, 0:1], R_, 'R', True)
                    if SCAN_SUB <= 1:
                        return
                    load_dup(padB, 'padB', 256 + h * 64, c0, T, 1)
                    shift(padB, 'padB', prm[:, 1:2], K_, 'K', True)
                    load_dup(padA, 'padA', 512 + h * 64, c0, T, 1)
                    shift(padA, 'padA', prm[:, 2:3], V_, 'V', True)
                    if SCAN_SUB <= 2:
                        return
                    zpad(padB, 'padB', T, 1)
                    fw.dma('sp', padB[:, 1:T + 1], Ps[768:896, c0:c0 + T], allP, ['padB'])
                    shift(padB, 'padB', prm[:, 10:11], X1, 'X1', False)
                    fw.op('act', ['X1'], ['X1'], lambda: nc.scalar.activation(X1[H0, 0:T], X1[H0, 0:T], AF.Tanh))
                    if SCAN_SUB <= 3:
                        return
                    for (o, n) in pieces(T):
                        ro = T - o - n
                        ps, pk = PS.next()
                        fw.op('pe', ['X1', 'w2c'], [pk], lambda: nc.tensor.matmul(
                            ps[:, 0:n], w2c[0:64, :], X1[H0, o:o + n], start=True, stop=True))
                        fw.op('act', [pk, 'prm'], ['t2'], lambda: nc.scalar.activation(
                            t2[:, o:o + n], ps[:, 0:n], AF.Sigmoid, bias=prm[:, 3:4], scale=1.0))
                        ps2, pk2 = PS.next()
                        fw.op('pe', ['X1', 'a2c'], [pk2], lambda: nc.tensor.matmul(
                            ps2[:, 0:n], a2c[64:128, :], X1[H1, o:o + n], start=True, stop=True), rt=1)
                        fw.op('act', [pk2, 'prm'], ['CSX'], lambda: nc.scalar.activation(
                            CSX[:, o:o + n], ps2[:, 0:n], AF.Sigmoid, bias=prm[:, 4:5], scale=1.0))
                    fw.op('dve', ['t2'], ['t1'], lambda: nc.vector.tensor_scalar(
                        t1[H0, 0:T], t2[H0, 0:T], -0.6065306597126334, None, ALU.mult))
                    fw.op('dve', ['t2'], ['t1'], lambda: nc.vector.tensor_scalar(
                        t1[H1, 0:T], t2[H1, 0:T][:, ::-1], -0.6065306597126334, None, ALU.mult))
                    fw.op('dve', ['CSX'], ['X2'], lambda: nc.vector.tensor_copy(X2[H0, 0:T], CSX[H0, 0:T]))
                    fw.op('dve', ['CSX'], ['X2'], lambda: nc.vector.tensor_copy(X2[H1, 0:T], CSX[H1, 0:T][:, ::-1]))
                    if SCAN_SUB <= 4:
                        return
                    zpad(padA, 'padA', T, 1)
                    fw.dma('sp', padA[:, 1:T + 1], Ps[896:1024, c0:c0 + T], allP, ['padA'])
                    shift(padA, 'padA', prm[:, 11:12], X1, 'X1', False)
                    fw.op('act', ['X1'], ['X1'], lambda: nc.scalar.activation(X1[:, 0:T], X1[:, 0:T], AF.Sigmoid))
                    for (o, n) in pieces(T):
                        ps, pk = PS.next()
                        fw.op('pe', ['X1', 'g2h'], [pk], lambda: nc.tensor.matmul(
                            ps[0:64, 0:n], g2h[:], X1[:, o:o + n], start=True, stop=True))
                        fw.op('act', [pk], ['GT'], lambda: nc.scalar.copy(GT[:, o:o + n], ps[0:64, 0:n]))
                    if SCAN_SUB <= 5:
                        return
                    fw.op('dve', ['K', 'prm'], ['A'], lambda: nc.vector.tensor_scalar(
                        A_[:, 0:T], K_[:, 0:T], prm[:, 5:6], None, ALU.mult))
                    fw.op('pool', ['A'], ['X1'], lambda: nc.gpsimd.tensor_tensor(X1[:, 0:T], A_[:, 0:T], A_[:, 0:T], ALU.mult))
                    sum_halves(X1, 'X1', T, B_, 'B', onesb[:], AF.Sqrt, EPS, 1.0)
                    fw.op('dve', ['B'], ['B'], lambda: nc.vector.reciprocal(B_[:, 0:T], B_[:, 0:T]))
                    fw.op('dve', ['A', 'B'], ['A'], lambda: nc.vector.tensor_tensor(A_[:, 0:T], A_[:, 0:T], B_[:, 0:T], ALU.mult))
                    fw.op('dve', ['A', 'X2'], ['B'], lambda: nc.vector.tensor_tensor(B_[:, 0:T], A_[:, 0:T], X2[:, 0:T], ALU.mult))
                    fw.op('dve', ['A'], ['A'], lambda: nc.vector.tensor_scalar(A_[:, 0:T], A_[:, 0:T], -1.0, None, ALU.mult))
                    if SCAN_SUB <= 6:
                        return
                    fw.op('dve', ['X2', 'prm'], ['X2'], lambda: nc.vector.tensor_scalar(
                        X2[:, 0:T], X2[:, 0:T], -1.0, prm[:, 6:7], ALU.add, ALU.mult))
                    fw.op('dve', ['X2', 'K'], ['K'], lambda: nc.vector.scalar_tensor_tensor(
                        K_[:, 0:T], X2[:, 0:T], 1.0, K_[:, 0:T], ALU.add, ALU.mult))
                    fw.op('dve', ['R', 'K'], ['X1'], lambda: nc.vector.tensor_tensor(X1[:, 0:T], R_[:, 0:T], K_[:, 0:T], ALU.mult))
                    fw.op('dve', ['X1', 'prm'], ['X2'], lambda: nc.vector.tensor_scalar(
                        X2[H0, 0:T], X1[H0, 0:T], prm[H0, 7:8], None, ALU.mult))
                    fw.op('dve', ['X1', 'prm'], ['X2'], lambda: nc.vector.tensor_scalar(
                        X2[H1, 0:T], X1[H1, 0:T][:, ::-1], prm[H1, 7:8], None, ALU.mult))
                    for (o, n) in pieces(T):
                        ps, pk = PS.next()
                        fw.op('pe', ['X2', 'ones'], [pk], lambda: nc.tensor.matmul(
                            ps[0:64, 0:n], ones[:, 0:64], X2[:, o:o + n], start=True, stop=True))
                        fw.op('dve', [pk, 'V'], ['BON'], lambda: nc.vector.tensor_tensor(
                            BON[:, o:o + n], ps[0:64, 0:n], V_[H0, o:o + n], ALU.mult))
                    if SCAN_SUB <= 7:
                        return
                    engine(T, 'vec', s0_ap, fin_ap)
                    o_ = X1
                    fw.op('dve', ['osc'], ['X1'], lambda: nc.vector.tensor_tensor(
                        o_[H0, 0:T], osc[:, 0, 0:T], osc[:, 1, 0:T][:, ::-1], ALU.add))
                    for (o, n) in pieces(T):
                        ps, pk = PS.next()
                        fw.op('pe', ['X1', 'ones'], [pk], lambda: nc.tensor.matmul(
                            ps[0:64, 0:n], ones[0:64, 0:64], o_[H0, o:o + n], start=True, stop=True))
                        fw.op('dve', [pk, 'X1'], ['X2'], lambda: nc.vector.scalar_tensor_tensor(
                            X2[H0, o:o + n], ps[0:64, 0:n], -1.0 / 64, o_[H0, o:o + n], ALU.mult, ALU.add))
                        fw.op('pool', ['X2'], ['t2'], lambda: nc.gpsimd.tensor_tensor(
                            t2[H0, o:o + n], X2[H0, o:o + n], X2[H0, o:o + n], ALU.mult))
                        ps2, pk2 = PS.next()
                        fw.op('pe', ['t2', 'ones'], [pk2], lambda: nc.tensor.matmul(
                            ps2[0:64, 0:n], ones[0:64, 0:64], t2[H0, o:o + n], start=True, stop=True))
                        fw.op('act', [pk2], ['t2'], lambda: nc.scalar.activation(
                            t2[H0, o:o + n], ps2[0:64, 0:n], AF.Sqrt, bias=64e-5, scale=1.0 / 64))
                    fw.op('dve', ['t2'], ['t2'], lambda: nc.vector.reciprocal(t2[H0, 0:T], t2[H0, 0:T]))
                    fw.op('dve', ['X2', 't2'], ['X2'], lambda: nc.vector.tensor_tensor(X2[H0, 0:T], X2[H0, 0:T], t2[H0, 0:T], ALU.mult))
                    fw.op('dve', ['X2', 'prm'], ['X2'], lambda: nc.vector.tensor_scalar(
                        X2[H0, 0:T], X2[H0, 0:T], prm[H0, 8:9], prm[H0, 9:10], ALU.mult, ALU.add))
                    fw.op('dve', ['X2', 'BON'], ['X2'], lambda: nc.vector.tensor_tensor(X2[H0, 0:T], X2[H0, 0:T], BON[:, 0:T], ALU.add))
                    fw.op('dve', ['X2', 'GT'], ['X2'], lambda: nc.vector.tensor_tensor(X2[H0, 0:T], X2[H0, 0:T], GT[:, 0:T], ALU.mult))
                    fw.dma('sp', Ys[h * 64:(h + 1) * 64, c0:c0 + T], X2[H0, 0:T], ['X2'], ycols)

                def unit_dn(c0, T, h, s0_ap, fin_ap, ycols):
                    fw.dma('sp', prm[:, 0:20], dnp_in[l, h], [], ['prm'])
                    fw.dma('sp', prm2[:, :], dnp2_in[l, h], [], ['prm2'])
                    H0, H1 = slice(0, 64), slice(64, 128)
                    fw.op('act', ['prm'], ['negA'], lambda: nc.scalar.activation(negA[:, 0:1], prm[:, 16:17], AF.Exp))
                    fw.op('dve', ['negA'], ['negA'], lambda: nc.vector.tensor_scalar(negA[:, 1:2], negA[:, 0:1], -1.0, None, ALU.mult))
                    fw.op('act', ['prm2'], ['negA2'], lambda: nc.scalar.activation(negA2[:, 0:2], prm2[:, 0:2], AF.Exp))
                    fw.op('dve', ['negA2'], ['negA2'], lambda: nc.vector.tensor_scalar(negA2[:, 0:2], negA2[:, 0:2], -1.0, None, ALU.mult))

                    def conv(pad, padk, w0, dst, dkey):
                        fw.op('dve', [padk, 'prm'], ['t2'], lambda: nc.vector.tensor_scalar(
                            t2[:, 0:T], pad[:, 0:T], prm[:, w0:w0 + 1], None, ALU.mult))
                        for j in range(1, 5):
                            fw.op('dve', [padk, 'prm', 't2'], ['t2'], lambda: nc.vector.scalar_tensor_tensor(
                                t2[:, 0:T], pad[:, j:j + T], prm[:, w0 + j:w0 + j + 1], t2[:, 0:T], ALU.mult, ALU.add))
                        fw.op('act', ['t2'], [dkey], lambda: nc.scalar.activation(dst[:, 0:T], t2[:, 0:T], AF.Silu))

                    def l2n(src, skey, dst, dkey, scale):
                        fw.op('pool', [skey], ['t2'], lambda: nc.gpsimd.tensor_tensor(t2[:, 0:T], src[:, 0:T], src[:, 0:T], ALU.mult))
                        sum_halves(t2, 't2', T, CSX, 'CSX', onesb[:], AF.Sqrt, EPS, 1.0)
                        fw.op('dve', ['CSX'], ['CSX'], lambda: nc.vector.reciprocal(CSX[:, 0:T], CSX[:, 0:T]))
                        fw.op('dve', [skey, 'CSX'], [dkey], lambda: nc.vector.scalar_tensor_tensor(
                            dst[H0, 0:T], src[H0, 0:T], scale, CSX[H0, 0:T], ALU.mult, ALU.mult))
                        fw.op('dve', [skey, 'CSX'], [dkey], lambda: nc.vector.scalar_tensor_tensor(
                            dst[H1, 0:T], src[H1, 0:T][:, ::-1], scale, CSX[H1, 0:T][:, ::-1], ALU.mult, ALU.mult))
                    load_dup(padA, 'padA', 1792 + h * 64, c0, T, 2)
                    conv(padA, 'padA', 0, X1, 'X1')
                    l2n(X1, 'X1', R_, 'R', 0.125)
                    load_dup(padB, 'padB', 2048 + h * 64, c0, T, 2)
                    conv(padB, 'padB', 5, X1, 'X1')
                    l2n(X1, 'X1', K_, 'K', 1.0)
                    load_dup(padA, 'padA', 2304 + h * 64, c0, T, 2)
                    conv(padA, 'padA', 10, X1, 'X1')
                    for d in range(2):
                        fw.dma('sp', padB[d * 64:(d + 1) * 64, 0:T],
                               Ps[2816 + d * 4 + h:2816 + d * 4 + h + 1, c0:c0 + T].to_broadcast([64, T]), allP, ['padB'])
                        fw.dma('sp', X2[d * 64:(d + 1) * 64, 0:T],
                               Ps[2824 + d * 4 + h:2824 + d * 4 + h + 1, c0:c0 + T].to_broadcast([64, T]), allP, ['X2'])
                        fw.dma('sp', csrow[:, d, 0:T],
                               Ps[2816 + d * 4 + h:2816 + d * 4 + h + 1, c0:c0 + T].to_broadcast([64, T]), allP, ['csrow'])
                    fw.op('act', ['padB', 'prm'], ['t2'], lambda: nc.scalar.activation(
                        t2[:, 0:T], padB[:, 0:T], AF.Exp, bias=prm[:, 17:18], scale=1.0))
                    fw.op('act', ['t2'], ['t2'], lambda: nc.scalar.activation(t2[:, 0:T], t2[:, 0:T], AF.Ln, bias=1.0, scale=1.0))
                    fw.op('dve', ['t2', 'negA'], ['t1'], lambda: nc.vector.tensor_scalar(
                        t1[H0, 0:T], t2[H0, 0:T], negA[H0, 1:2], None, ALU.mult))
                    fw.op('dve', ['t2', 'negA'], ['t1'], lambda: nc.vector.tensor_scalar(
                        t1[H1, 0:T], t2[H1, 0:T][:, ::-1], negA[H1, 1:2], None, ALU.mult))
                    for d in range(2):
                        fw.op('act', ['csrow', 'prm2'], ['csrow'], lambda: nc.scalar.activation(
                            csrow[:, d, 0:T], csrow[:, d, 0:T], AF.Exp, bias=prm2[:, 2 + d:3 + d], scale=1.0))
                        fw.op('act', ['csrow'], ['csrow'], lambda: nc.scalar.activation(
                            csrow[:, d, 0:T], csrow[:, d, 0:T], AF.Ln, bias=1.0, scale=1.0))
                    fw.op('dve', ['csrow', 'negA2'], ['t2'], lambda: nc.vector.tensor_scalar(
                        t2[0:64, 0:T], csrow[:, 1, 0:T], negA2[:, 1:2], None, ALU.mult))
                    fw.op('dve', ['csrow', 'negA2'], ['csrow'], lambda: nc.vector.tensor_scalar(
                        csrow[:, 0, 0:T], csrow[:, 0, 0:T], negA2[:, 0:1], None, ALU.mult))
                    fw.op('dve', ['t2'], ['csrow'], lambda: nc.vector.tensor_copy(csrow[:, 1, 0:T], t2[0:64, 0:T][:, ::-1]))
                    for d in range(2):
                        for b in range(T // 256):
                            sl = slice(b * 256, (b + 1) * 256)
                            fw.op('dve', ['csrow'], ['t2'], lambda: nc.vector.tensor_tensor_scan(
                                t2[0:64, sl], csrow[:, d, sl], csrow[:, d, sl], 0.0, ALU.add, ALU.bypass))
                            fw.op('dve', ['csrow', 't2'], ['cendc'], lambda: nc.vector.tensor_tensor(
                                cendc[0:64, :].unsqueeze(2), t2[0:64, sl].rearrange("p (c i) -> p c i", i=64)[:, :, 0:1],
                                csrow[:, d, sl].rearrange("p (c i) -> p c i", i=64)[:, :, 0:1], ALU.subtract))
                            fw.op('dve', ['cendc', 't2'], ['csrow'], lambda: nc.vector.tensor_tensor(
                                csrow[:, d, sl].rearrange("p (c i) -> p c i", i=64),
                                t2[0:64, sl].rearrange("p (c i) -> p c i", i=64),
                                cendc[0:64, :].unsqueeze(2).to_broadcast([64, 4, 64]), ALU.subtract))
                    fw.op('act', ['X2'], ['X2'], lambda: nc.scalar.activation(X2[:, 0:T], X2[:, 0:T], AF.Sigmoid))
                    fw.op('dve', ['X2', 'X1'], ['V'], lambda: nc.vector.tensor_tensor(V_[H0, 0:T], X1[H0, 0:T], X2[H0, 0:T], ALU.mult))
                    fw.op('dve', ['X2', 'X1'], ['V'], lambda: nc.vector.tensor_tensor(
                        V_[H1, 0:T], X1[H1, 0:T][:, ::-1], X2[H1, 0:T][:, ::-1], ALU.mult))
                    fw.op('dve', ['X2'], ['CSX'], lambda: nc.vector.tensor_copy(CSX[H0, 0:T], X2[H0, 0:T]))
                    fw.op('dve', ['X2'], ['CSX'], lambda: nc.vector.tensor_copy(CSX[H1, 0:T], X2[H1, 0:T][:, ::-1]))
                    fw.op('dve', ['CSX', 'K'], ['A'], lambda: nc.vector.scalar_tensor_tensor(
                        A_[:, 0:T], CSX[:, 0:T], -1.0, K_[:, 0:T], ALU.mult, ALU.mult))
                    fw.op('pool', ['K'], ['B'], lambda: nc.gpsimd.tensor_copy(B_[:, 0:T], K_[:, 0:T]))
                    fw.dma('sp', GT[:, 0:T], Ps[2560 + h * 64:2624 + h * 64, c0:c0 + T], allP, ['GT'])
                    fw.op('act', ['GT'], ['GT'], lambda: nc.scalar.activation(GT[:, 0:T], GT[:, 0:T], AF.Silu))
                    engine(T, 'scal', s0_ap, fin_ap)
                    o_ = X1
                    fw.op('dve', ['osc'], ['X1'], lambda: nc.vector.tensor_tensor(
                        o_[H0, 0:T], osc[:, 0, 0:T], osc[:, 1, 0:T][:, ::-1], ALU.add))
                    fw.op('pool', ['X1'], ['t2'], lambda: nc.gpsimd.tensor_tensor(t2[H0, 0:T], o_[H0, 0:T], o_[H0, 0:T], ALU.mult))
                    for (o, n) in pieces(T):
                        ps, pk = PS.next()
                        fw.op('pe', ['t2', 'ones'], [pk], lambda: nc.tensor.matmul(
                            ps[0:64, 0:n], ones[0:64, 0:64], t2[H0, o:o + n], start=True, stop=True))
                        fw.op('act', [pk], ['X2'], lambda: nc.scalar.activation(
                            X2[H0, o:o + n], ps[0:64, 0:n], AF.Sqrt, bias=EPS, scale=1.0 / 64))
                    fw.op('dve', ['X2'], ['X2'], lambda: nc.vector.reciprocal(X2[H0, 0:T], X2[H0, 0:T]))
                    fw.op('dve', ['X2', 'X1', 'prm'], ['X2'], lambda: nc.vector.scalar_tensor_tensor(
                        X2[H0, 0:T], o_[H0, 0:T], prm[H0, 15:16], X2[H0, 0:T], ALU.mult, ALU.mult))
                    fw.op('dve', ['X2', 'GT'], ['X2'], lambda: nc.vector.tensor_tensor(X2[H0, 0:T], X2[H0, 0:T], GT[:, 0:T], ALU.mult))
                    fw.dma('sp', Ys[768 + h * 64:832 + h * 64, c0:c0 + T], X2[H0, 0:T], ['X2'], ycols)

                allP = [('P', t) for t in range(NTILE)]
                unit = unit_rwkv if which == 'rwkv' else unit_dn
                mi = 0 if which == 'rwkv' else 1
                st_in = srw_in if which == 'rwkv' else sdn_in
                if SCAN_TEST:
                    unit(0, 256, 1, None, st_out[l, 0, mi, 1], [('Y', 0)])
                    if SCAN_TEST[1] >= 5:
                        unit(512, 2048, 2, st_in[l, 2], None, [('Y', t) for t in range(1, NTILE)])
                else:
                    for s in range(2):
                        for h in range(4):
                            unit(s * 256, 256, h, None, st_out[l, s, mi, h], [('Y', 0)])
                    for h in range(4):
                        unit(512, 2048, h, st_in[l, h], None, [('Y', t) for t in range(1, NTILE)])
                fw.barrier()

        def mixers(l):
            with ExitStack() as ms:
                msb = mk_sb(ms)
                if not ENABLE_RWKV:
                    zero_rows(msb, 0, 256)
                if not ENABLE_ATTN:
                    zero_rows(msb, 256, 768)
                if not ENABLE_DN:
                    zero_rows(msb, 768, 1024)
                fw.barrier()
            if ENABLE_ATTN:
                attention_phase(l)
            if ENABLE_RWKV:
                scan_phase(l, 'rwkv')
            if ENABLE_DN:
                scan_phase(l, 'dn')
            if DEBUG and l == 0:
                fw.dma('sp', dbgY[:, :], Ys[:, :], [('Y', t) for t in range(NTILE)], ['dbgY'])
                fw.dma('sp', dbgP[:, :], Ps[:, :], [('P', t) for t in range(NTILE)], ['dbgP'])

        if SCAN_TEST:
            for t in range(NTILE):
                fw.dma('sp', Ps[:, t * TT:(t + 1) * TT], ptest[:, t * TT:(t + 1) * TT], [], [('P', t)])
                fw.dma('sp', Ys[:, t * TT:(t + 1) * TT], ptest[0:D, t * TT:(t + 1) * TT], [], [('Y', t)])
            fw.barrier()
            scan_phase(0, SCAN_TEST[0])
            fw.dma('sp', yT_out[:, :], Ys[:, :], [('Y', t) for t in range(NTILE)], ['yout'])
        for l in range(0 if SCAN_TEST else N_LAYERS_BUILD):
            trunk_phase(l, True)
            mixers(l)
            trunk_phase(l, False)
        fw.finish('sp')
        print("instructions:", fw.nins, fw.cnt)
    return nc


_NC = None


def _rope_tables():
    t = np.arange(2048)
    row = (t // 64).astype(np.float32)
    col = (t % 64).astype(np.float32)
    freqs = (10000.0 ** (-np.arange(16, dtype=np.float32) / 16)).astype(np.float32)
    cos = np.zeros((64, 2048), np.float32)
    sin = np.zeros((64, 2048), np.float32)
    for d in range(64):
        pos = row if d < 32 else col
        ang = (pos * freqs[d % 16]).astype(np.float32)
        cos[d] = np.cos(ang)
        sin[d] = np.sin(ang)
    R = np.zeros((64, 64), np.float32)
    for base in (0, 32):
        for i in range(16):
            R[base + i, base + i + 16] = -1.0
            R[base + 16 + i, base + i] = 1.0
    return cos, sin, np.ascontiguousarray(R.T)


def _attn_masks():
    qi = np.arange(128)[:, None]
    kj = np.arange(384)[None, :]
    band = np.abs(kj - 128 - qi) <= 128
    m = np.zeros((128, 3, 384), np.float32)
    for var, n in ((0, 0), (1, 5), (2, 15)):
        key_pos = (n - 1) * 128 + kj
        ok = band & (key_pos >= 0) & (key_pos < 2048)
        m[:, var, :] = np.where(ok, 0.0, NEG)
    return m


def kernel(x_prompt, x_sample, cache_attn_k, cache_attn_v, state_rwkv, state_delta, c, c_ctx,
           norm_w, ada_w, ada_b, ffn_w_in, ffn_w_out, w_in, w_out,
           rwkv_mu, rwkv_w0, rwkv_w2, rwkv_a0, rwkv_a2, rwkv_g2, rwkv_kk, rwkv_ka, rwkv_rk,
           rwkv_lnx_w, rwkv_lnx_b, attn_sink, dn_conv, dn_A_log, dn_dt_bias, dn_norm_w, final_norm_w):
    global _NC
    f = lambda a: np.ascontiguousarray(np.asarray(a, dtype=np.float32))
    x_prompt, x_sample = f(x_prompt), f(x_sample)
    cache_attn_k, cache_attn_v = f(cache_attn_k), f(cache_attn_v)
    if _NC is None:
        _NC = build_program()
    nc = _NC
    fm = lambda v: f(np.asarray(v).reshape(-1, 128).T)
    cos, sin, rotT = _rope_tables()
    shared = {
        "ada_w": f(ada_w),
        "ada_bT": f(np.stack([fm(ada_b[l]) for l in range(L)])),
        "norm_wT": f(np.stack([np.concatenate([fm(norm_w[l, n]) for n in range(3)], axis=1) for l in range(L)])),
        "fnorm_wT": fm(final_norm_w),
        "ffn_w_in": f(ffn_w_in), "ffn_w_out": f(ffn_w_out), "w_in": f(w_in), "w_out": f(w_out),
        "ones_c": np.ones((128, 128), np.float32),
        "ident_c": np.eye(128, dtype=np.float32),
        "cos_c": cos, "sin_c": sin, "rotT_c": rotT,
        "amask_c": _attn_masks(),
        "g2": f(rwkv_g2),
        "mstr_c": f(np.tile(np.triu(np.ones((64, 64), np.float32), 1), (1, 8))),
        "minc_c": f(np.tile(np.triu(np.ones((64, 64), np.float32), 0), (1, 8))),
        "id8_c": f(np.tile(np.eye(64, dtype=np.float32), (1, 8))),
        "onesb_c": f(np.kron(np.eye(2, dtype=np.float32), np.ones((64, 64), np.float32))),
        "onehot_c": f(np.eye(128, dtype=np.float32)[:, 0:1]),
        "sinkb": f(np.broadcast_to(np.asarray(attn_sink, np.float32)[:, None, :], (L, 128, 8))),
    }
    A = lambda v: np.asarray(v, np.float32)
    rwp = np.zeros((L, 4, 128, 12), np.float32)
    w2cat = np.zeros((L, 4, 64, 128), np.float32)
    a2cat = np.zeros((L, 4, 64, 128), np.float32)
    dnp = np.zeros((L, 4, 128, 20), np.float32)
    dnp2 = np.zeros((L, 4, 64, 4), np.float32)
    mu, w0, a0 = A(rwkv_mu), A(rwkv_w0), A(rwkv_a0)
    conv, Alog, dtb = A(dn_conv), A(dn_A_log), A(dn_dt_bias)
    for l in range(L):
        for h in range(4):
            js = slice(h * 64, (h + 1) * 64)
            for d in range(2):
                ps_ = slice(d * 64, (d + 1) * 64)
                rwp[l, h, ps_, 0] = mu[l, 0:256][js]
                rwp[l, h, ps_, 1] = mu[l, 256:512][js]
                rwp[l, h, ps_, 2] = mu[l, 512:768][js]
                rwp[l, h, ps_, 3] = w0[l, d][js]
                rwp[l, h, ps_, 4] = a0[l, d][js]
                rwp[l, h, ps_, 5] = A(rwkv_kk)[l][js]
                rwp[l, h, ps_, 6] = A(rwkv_ka)[l][js]
                rwp[l, h, ps_, 7] = A(rwkv_rk)[l][js]
                rwp[l, h, ps_, 8] = A(rwkv_lnx_w)[l][js]
                rwp[l, h, ps_, 9] = A(rwkv_lnx_b)[l][js]
                w2cat[l, h, :, ps_] = A(rwkv_w2)[l, d][:, js]
                a2cat[l, h, :, ps_] = A(rwkv_a2)[l, d][:, js]
                for t in range(5):
                    dnp[l, h, ps_, t] = conv[l, t, 0:256][js]
                    dnp[l, h, ps_, 5 + t] = conv[l, t, 256:512][js]
                    dnp[l, h, ps_, 10 + t] = conv[l, t, 512:768][js]
                dnp[l, h, ps_, 15] = A(dn_norm_w)[l]
                dnp[l, h, ps_, 16] = Alog[l, d, h]
                dnp[l, h, ps_, 17] = dtb[l, d, h]
                dnp2[l, h, :, d] = Alog[l, d, h]
                dnp2[l, h, :, 2 + d] = dtb[l, d, h]
            rwp[l, h, :, 10] = mu[l, 768:896]
            rwp[l, h, :, 11] = mu[l, 896:1024]
    shared.update({"rwp": rwp, "w2cat": w2cat, "a2cat": a2cat, "dnp": dnp, "dnp2": dnp2})
    state_rwkv, state_delta = A(state_rwkv), A(state_delta)
    in_maps = []
    for core in range(8):
        b = core % 2
        xs = np.concatenate([x_prompt[2 * core].T, x_prompt[2 * core + 1].T, x_sample[b].T], axis=1)
        condT = np.stack([fm(c_ctx), fm(np.asarray(c)[b])], axis=-1)
        m = dict(shared)
        m["xT"] = f(xs)
        m["condT"] = f(condT)
        m["ck"] = f(cache_attn_k[b].reshape(L, 512, 128))
        m["cv"] = f(cache_attn_v[b].reshape(L, 512, 128))
        m["srw"] = f(state_rwkv[b].transpose(0, 2, 1, 4, 3).reshape(L, 4, 128, 64))
        m["sdn"] = f(state_delta[b].transpose(0, 2, 1, 3, 4).reshape(L, 4, 128, 64))
        in_maps.append(m)
    if SCAN_TEST:
        for m in in_maps:
            for k_ in ("ffn_w_in", "ffn_w_out"):
                m[k_] = np.zeros((1, 1, 128, 128), np.float32)
            for k_ in ("ada_w", "w_in", "w_out"):
                m[k_] = np.zeros((1, 128, 128), np.float32)
    res = run_bass_kernel_spmd(nc, in_maps, core_ids=list(range(8)))
    r = res.results
    global _LAST
    _LAST = r
    y_prompt = np.stack([r[s // 2]["yT"][:, (s % 2) * 256:(s % 2 + 1) * 256].T for s in range(16)])
    y_sample = np.stack([r[b]["yT"][:, 512:].T for b in range(2)])
    new_k = np.zeros((16, L, 256, 2, 64), np.float32)
    new_v = np.zeros((16, L, 256, 2, 64), np.float32)
    for s in range(16):
        kv = r[s // 2]["kvT"][:, :, (s % 2) * 256:(s % 2 + 1) * 256]
        new_k[s] = kv[:, 0:128, :].transpose(0, 2, 1).reshape(L, 256, 2, 64)
        new_v[s] = kv[:, 128:256, :].transpose(0, 2, 1).reshape(L, 256, 2, 64)
    new_sr = np.zeros((16, L, 2, 4, 64, 64), np.float32)
    new_sd = np.zeros((16, L, 2, 4, 64, 64), np.float32)
    for s in range(16):
        st = r[s // 2]["st"][:, s % 2]
        st = st.reshape(L, 2, 4, 2, 64, 64)
        new_sr[s] = st[:, 0].transpose(0, 2, 1, 4, 3)
        new_sd[s] = st[:, 1].transpose(0, 2, 1, 3, 4)
    return (np.ascontiguousarray(y_prompt, dtype=np.float32), np.ascontiguousarray(y_sample, dtype=np.float32),
            new_k, new_v, new_sr, new_sd)
```

```python
import numpy as np
from contextlib import ExitStack
import concourse.bass as bass
import concourse.mybir as mybir
from concourse.bass_utils import run_bass_kernel_spmd

F32 = mybir.dt.float32
BF16 = mybir.dt.bfloat16
ALU = mybir.AluOpType
AF = mybir.ActivationFunctionType

D = 1024
L = 4
DFF = 2816
NT = 2560
TT = 512
NTILE = NT // TT
PROJ_W = 2832
NPC = 23
EPS = 1e-6
N_LAYERS_BUILD = L


class FW:
    def __init__(self, nc, es, n_dma_slots=32):
        self.nc = nc
        self.eng = {'pe': nc.tensor, 'act': nc.scalar, 'dve': nc.vector, 'pool': nc.gpsimd, 'sp': nc.sync}
        self.sem = {e: es.enter_context(nc.semaphore('s_' + e)) for e in self.eng}
        self.cnt = {e: 0 for e in self.eng}
        self.seen = {e: {} for e in self.eng}
        self.dslots = [es.enter_context(nc.semaphore('d%d' % i)) for i in range(n_dma_slots)]
        self.dcnt = [0] * n_dma_slots
        self.dnext = 0
        self.dnext2 = [0, 0]
        self.lastw = {}
        self.reads = {}
        self.nins = 0
        self.rt = {}

    def _wait(self, e, prod):
        if prod is None:
            return
        kind, idx, val = prod
        if kind == 'e' and idx == e and e in ('pe', 'sp'):
            return
        k = (kind, idx)
        if self.seen[e].get(k, 0) >= val:
            return
        sem = self.sem[idx] if kind == 'e' else self.dslots[idx]
        self.eng[e].wait_ge(sem, val)
        self.seen[e][k] = val

    def _deps(self, e, reads, writes):
        for k in reads:
            self._wait(e, self.lastw.get(k))
        for k in writes:
            self._wait(e, self.lastw.get(k))
            for p in self.reads.get(k, {}).values():
                self._wait(e, p)

    def _record(self, prod, reads, writes):
        for k in reads:
            self.reads.setdefault(k, {})[(prod[0], prod[1])] = prod
        for k in writes:
            self.lastw[k] = prod
            self.reads[k] = {}

    def op(self, e, reads, writes, fn, rt=0, inc=True):
        self._deps(e, reads, writes)
        if e == 'pe':
            for k in writes:
                if self.rt.get(k, 0) != rt and self.cnt['pe']:
                    if self.seen['pe'].get(('e', 'pe'), 0) < self.cnt['pe']:
                        self.eng['pe'].wait_ge(self.sem['pe'], self.cnt['pe'])
                        self.seen['pe'][('e', 'pe')] = self.cnt['pe']
                self.rt[k] = rt
        ins = fn()
        self.nins += 1
        if inc:
            self.cnt[e] += 1
            ins.then_inc(self.sem[e], 1)
            tok = self.cnt[e]
        else:
            tok = self.cnt[e] + 1
        self._record(('e', e, tok), reads, writes)
        return ins

    def dma(self, q, out, in_, reads, writes, **kw):
        half = len(self.dslots) // 2
        qi = 1 if q == 'pool' else 0
        s = qi * half + self.dnext2[qi]
        self.dnext2[qi] = (self.dnext2[qi] + 1) % half
        self._wait(q, ('d', s, self.dcnt[s]) if self.dcnt[s] else None)
        self._deps(q, reads, writes)
        ins = self.eng[q].dma_start(out=out, in_=in_, **kw)
        self.dcnt[s] += 16
        self.nins += 1
        ins.then_inc(self.dslots[s], 16)
        self._record(('d', s, self.dcnt[s]), reads, writes)
        return ins

    def barrier(self):
        for e in self.eng:
            for f in self.eng:
                if self.cnt[f]:
                    self._wait(e, ('e', f, self.cnt[f]))
            for i in range(len(self.dslots)):
                if self.dcnt[i]:
                    self._wait(e, ('d', i, self.dcnt[i]))

    def finish(self, e='sp'):
        for k, p in list(self.lastw.items()):
            self._wait(e, p)
        for k, d in list(self.reads.items()):
            for p in d.values():
                self._wait(e, p)


class Ring:
    def __init__(self, tiles, name):
        self.tiles = tiles
        self.name = name
        self.i = 0

    def next(self):
        t = self.tiles[self.i]
        k = '%s%d' % (self.name, self.i)
        self.i = (self.i + 1) % len(self.tiles)
        return t, k


ENABLE_ATTN = True
DEBUG = False
SCAN_SUB = 99
SCAN_TEST = None
_LAST = None
ENABLE_RWKV = True
ENABLE_DN = True
NEG = -1e30


def build_program():
    nc = bass.Bass("TRN2", target_bir_lowering=False)
    din = lambda name, shape: nc.dram_tensor(name, shape, F32, kind="ExternalInput").ap()
    dout = lambda name, shape: nc.dram_tensor(name, shape, F32, kind="ExternalOutput").ap()
    xT_in = din("xT", [D, NT])
    condT = din("condT", [128, 8, 2])
    if SCAN_TEST:
        _din = din
        din = lambda name, shape: _din(name, [1, 1, 128, 128] if name in ("ffn_w_in", "ffn_w_out") else ([1, 128, 128] if name in ("ada_w", "w_in", "w_out") else shape))
    ada_w = din("ada_w", [L, D, 9 * D])
    ada_bT = din("ada_bT", [L, 128, 72])
    norm_wT = din("norm_wT", [L, 128, 24])
    fnorm_wT = din("fnorm_wT", [128, 8])
    ffn_w_in = din("ffn_w_in", [L, 2, D, 2 * DFF])
    ffn_w_out = din("ffn_w_out", [L, 2, DFF, D])
    w_in = din("w_in", [L, D, PROJ_W])
    w_out = din("w_out", [L, D, D])
    ones_c = din("ones_c", [128, 128])
    ident_c = din("ident_c", [128, 128])
    cos_c = din("cos_c", [64, 2048])
    sin_c = din("sin_c", [64, 2048])
    rotT_c = din("rotT_c", [64, 64])
    amask_c = din("amask_c", [128, 3, 384])
    sinkb = din("sinkb", [L, 128, 8])
    ck_in = din("ck", [L, 512, 128])
    cv_in = din("cv", [L, 512, 128])
    rwp_in = din("rwp", [L, 4, 128, 12])
    w2cat_in = din("w2cat", [L, 4, 64, 128])
    a2cat_in = din("a2cat", [L, 4, 64, 128])
    g2_in = din("g2", [L, 128, 256])
    dnp_in = din("dnp", [L, 4, 128, 20])
    dnp2_in = din("dnp2", [L, 4, 64, 4])
    srw_in = din("srw", [L, 4, 128, 64])
    sdn_in = din("sdn", [L, 4, 128, 64])
    mstr_c = din("mstr_c", [64, 512])
    minc_c = din("minc_c", [64, 512])
    id8_c = din("id8_c", [64, 512])
    onesb_c = din("onesb_c", [128, 128])
    onehot_c = din("onehot_c", [128, 1])
    st_out = dout("st", [L, 2, 2, 4, 128, 64])
    yT_out = dout("yT", [D, NT])
    dbgY = dout("dbgY", [D, NT]) if DEBUG else None
    dbgP = dout("dbgP", [NPC * 128, NT]) if DEBUG else None
    ptest = din("ptest", [NPC * 128, NT]) if SCAN_TEST else None
    kvT_out = dout("kvT", [L, 256, 512])

    Xs = nc.dram_tensor("Xs", [D, NT], F32).ap()
    Ps = nc.dram_tensor("Ps", [NPC * 128, NT], F32).ap()
    Ys = nc.dram_tensor("Ys", [D, NT], F32).ap()

    with ExitStack() as es:
        fw = FW(nc, es)
        psb = [es.enter_context(nc.psum_tensor("psb%d" % i, [128, 512], F32)) for i in range(8)]
        PS = Ring(psb, 'ps')
        uid = [0]

        def mk_sb(stack):
            def sb(name, shape, dt=F32):
                uid[0] += 1
                return stack.enter_context(nc.sbuf_tensor("%s_%d" % (name, uid[0]), shape, dt))
            return sb
        sb = mk_sb(es)

        ones = sb("ones", [128, 128])
        fw.dma('sp', ones[:], ones_c[:, :], [], ['ones'])
        ident = sb("ident", [128, 128])
        fw.dma('sp', ident[:], ident_c[:, :], [], ['ident'])
        onesb = sb("onesb", [128, 128])
        fw.dma('sp', onesb[:], onesb_c[:, :], [], ['onesb'])
        onehot = sb("onehot", [128, 1])
        fw.dma('sp', onehot[:], onehot_c[:, :], [], ['onehot'])
        cond = sb("cond", [128, 8, 2])
        fw.dma('sp', cond[:], condT[:, :, :], [], ['cond'])
        scond = sb("scond", [128, 8, 2], BF16)
        fw.op('act', ['cond'], ['scond'], lambda: nc.scalar.activation(scond[:], cond[:], AF.Silu))
        fnw = sb("fnw", [128, 8])
        fw.dma('sp', fnw[:], fnorm_wT[:, :], [], ['fnw'])
        modT = sb("modT", [128, 72, 2])
        nwt = sb("nwt", [128, 24])
        modA = sb("modA", [128, 24, 2])
        adab = sb("adab", [128, 72])

        def cj_of(tile):
            return 0 if tile == 0 else 1

        def trunk_phase(l, first):
            with ExitStack() as ts:
                tsb = mk_sb(ts)
                xt = tsb("xt", [128, 8, TT])
                sq = tsb("sq", [128, 8, TT])
                rstd = tsb("rstd", [128, TT])
                hT = tsb("hT", [128, 8, TT], BF16)
                actT = tsb("actT", [128, 22, TT], BF16)
                sg = Ring([tsb("sg%d" % i, [128, TT]) for i in range(2)], 'sg')
                yt = tsb("yt", [128, 8, TT], BF16)
                pt = Ring([tsb("pt%d" % i, [128, TT]) for i in range(3)], 'pt')
                wA = Ring([tsb("wA%d" % i, [128, 8, 512], BF16) for i in range(4)], 'wA')
                wB = Ring([tsb("wB%d" % i, [128, 8, 512], BF16) for i in range(4)], 'wB')
                wO = Ring([tsb("wO%d" % i, [128, 4, 1024], BF16) for i in range(4)], 'wO')

                def adaln():
                    fw.dma('sp', adab[:], ada_bT[l], [], ['adab'])
                    fw.dma('sp', nwt[:], norm_wT[l], [], ['nwt'])
                    ps, pk = PS.next()
                    src = ada_w[l].rearrange("(k p) f -> p k f", p=128)
                    for blk in range(18):
                        wt, wk = wA.next()
                        fw.dma('pool', wt[:], src[:, :, blk * 512:(blk + 1) * 512], [], [wk])
                        for mm in range(4):
                            m = blk * 4 + mm
                            for k in range(8):
                                fw.op('pe', [wk, 'scond'], [pk], lambda: nc.tensor.matmul(
                                    ps[:, 2 * m:2 * m + 2], wt[:, k, mm * 128:(mm + 1) * 128], scond[:, k, :],
                                    start=(k == 0), stop=(k == 7)), inc=(k == 7))
                    fw.op('dve', [pk, 'adab'], ['modT'], lambda: nc.vector.tensor_tensor(
                        modT[:], ps[:, 0:144].rearrange("p (m j) -> p m j", j=2),
                        adab[:].unsqueeze(2).to_broadcast([128, 72, 2]), ALU.add))
                    for n in range(3):
                        sc = modT[:, (3 * n + 1) * 8:(3 * n + 2) * 8, :]
                        fw.op('dve', ['modT', 'nwt'], ['modA'], lambda: nc.vector.scalar_tensor_tensor(
                            modA[:, n * 8:(n + 1) * 8, :], sc, 1.0,
                            nwt[:, n * 8:(n + 1) * 8].unsqueeze(2).to_broadcast([128, 8, 2]), ALU.add, ALU.mult))

                def load_x(from_input, tile):
                    src = (xT_in if from_input else Xs)
                    fw.dma('sp', xt[:], src.rearrange("(k p) t -> p k t", p=128)[:, :, tile * TT:(tile + 1) * TT],
                           [('X', tile)], ['xt'])

                def store_x(tile):
                    fw.dma('sp', Xs.rearrange("(k p) t -> p k t", p=128)[:, :, tile * TT:(tile + 1) * TT], xt[:],
                           ['xt'], [('X', tile)])

                def rms_rstd():
                    fw.op('act', ['xt'], ['sq'], lambda: nc.scalar.activation(sq[:], xt[:], AF.Square))
                    ps, pk = PS.next()
                    for k in range(8):
                        fw.op('pe', ['sq', 'ones'], [pk], lambda: nc.tensor.matmul(
                            ps[:, :], ones[:], sq[:, k, :], start=(k == 0), stop=(k == 7)), inc=(k == 7))
                    fw.op('act', [pk], ['rstd'], lambda: nc.scalar.activation(
                        rstd[:], ps[:, :], AF.Sqrt, bias=EPS, scale=1.0 / D))
                    fw.op('dve', ['rstd'], ['rstd'], lambda: nc.vector.reciprocal(rstd[:], rstd[:]))

                def modulate(n, cj):
                    rms_rstd()
                    fw.op('dve', ['xt', 'rstd'], ['sq'], lambda: nc.vector.tensor_tensor(
                        sq[:], xt[:], rstd[:].unsqueeze(1).to_broadcast([128, 8, TT]), ALU.mult))
                    for k in range(8):
                        e = 'dve' if k % 2 == 0 else 'pool'
                        eng = nc.vector if k % 2 == 0 else nc.gpsimd
                        fw.op(e, ['sq', 'modA', 'modT'], ['hT'], lambda: eng.tensor_scalar(
                            hT[:, k, :], sq[:, k, :], modA[:, n * 8 + k, cj:cj + 1],
                            modT[:, (3 * n) * 8 + k, cj:cj + 1], ALU.mult, ALU.add))

                def ffn(which, n, gidx, cj, gscale):
                    modulate(n, cj)
                    wi = ffn_w_in[l, which].rearrange("(k p) f -> p k f", p=128)
                    for g in range(6):
                        nch = 4 if g < 5 else 2
                        wa, wak = wA.next()
                        wb, wbk = wB.next()
                        fw.dma('pool', wa[:, :, 0:nch * 128], wi[:, :, g * 512:g * 512 + nch * 128], [], [wak])
                        fw.dma('pool', wb[:, :, 0:nch * 128], wi[:, :, DFF + g * 512:DFF + g * 512 + nch * 128],
                               [], [wbk])
                        for c in range(nch):
                            f = g * 4 + c
                            pg, pgk = PS.next()
                            pu, puk = PS.next()
                            for k in range(8):
                                fw.op('pe', [wak, 'hT'], [pgk], lambda: nc.tensor.matmul(
                                    pg[:, :], wa[:, k, c * 128:(c + 1) * 128], hT[:, k, :],
                                    start=(k == 0), stop=(k == 7)), inc=(k == 7))
                            for k in range(8):
                                fw.op('pe', [wbk, 'hT'], [puk], lambda: nc.tensor.matmul(
                                    pu[:, :], wb[:, k, c * 128:(c + 1) * 128], hT[:, k, :],
                                    start=(k == 0), stop=(k == 7)), inc=(k == 7))
                            s, sk = sg.next()
                            fw.op('act', [pgk], [sk], lambda: nc.scalar.activation(s[:], pg[:, :], AF.Silu))
                            fw.op('dve', [sk, puk], [('actT', f)], lambda: nc.vector.tensor_tensor(
                                actT[:, f, :], s[:], pu[:, :], ALU.mult))
                    wo = ffn_w_out[l, which].rearrange("(f p) d -> p f d", p=128)
                    for g in range(6):
                        nch = 4 if g < 5 else 2
                        wt, wk = wO.next()
                        fw.dma('pool', wt[:, 0:nch, :], wo[:, g * 4:g * 4 + nch, :], [], [wk])
                        for c in range(nch):
                            f = g * 4 + c
                            for j in range(8):
                                fw.op('pe', [wk, ('actT', f)], ['ps%d' % j], lambda: nc.tensor.matmul(
                                    psb[j][:, :], wt[:, c, j * 128:(j + 1) * 128], actT[:, f, :],
                                    start=(f == 0), stop=(f == 21)))
                    for j in range(8):
                        gcol = modT[:, gidx * 8 + j, cj:cj + 1]
                        t, tk = pt.next()
                        fw.op('act', ['ps%d' % j, 'modT'], [tk], lambda: nc.scalar.activation(
                            t[:], psb[j][:, :], AF.Copy, scale=gcol))
                        fw.op('dve', [tk, 'xt'], ['xt'], lambda: nc.vector.scalar_tensor_tensor(
                            xt[:, j, :], t[:], gscale, xt[:, j, :], ALU.mult, ALU.add))

                def proj(tile, cj):
                    modulate(1, cj)
                    wi = w_in[l].rearrange("(k p) f -> p k f", p=128)
                    for g in range(6):
                        ncol = 512 if g < 5 else PROJ_W - 2560
                        wa, wak = wA.next()
                        fw.dma('pool', wa[:, :, 0:ncol], wi[:, :, g * 512:g * 512 + ncol], [], [wak])
                        for c in range((ncol + 127) // 128):
                            m = min(128, ncol - c * 128)
                            f = g * 4 + c
                            ps, pk = PS.next()
                            for k in range(8):
                                fw.op('pe', [wak, 'hT'], [pk], lambda: nc.tensor.matmul(
                                    ps[0:m, :], wa[:, k, c * 128:c * 128 + m], hT[:, k, :],
                                    start=(k == 0), stop=(k == 7)), inc=(k == 7))
                            t, tk = pt.next()
                            fw.op('act', [pk], [tk], lambda: nc.scalar.copy(t[0:m, :], ps[0:m, :]))
                            fw.dma('sp', Ps[f * 128:f * 128 + m, tile * TT:(tile + 1) * TT], t[0:m, :],
                                   [tk], [('P', tile, f)])
                            if tile == 0 and f in (12, 13):
                                r0 = 0 if f == 12 else 128
                                fw.dma('sp', kvT_out[l, r0:r0 + 128, :], t[:, :], [tk], ['kvout'])

                def mix_out(tile, cj):
                    fw.dma('pool', yt[:], Ys.rearrange("(k p) t -> p k t", p=128)[:, :, tile * TT:(tile + 1) * TT],
                           [('Y', tile)], ['yt'])
                    wo = w_out[l].rearrange("(f p) d -> p f d", p=128)
                    for g in range(2):
                        wt, wk = wO.next()
                        fw.dma('pool', wt[:, :, :], wo[:, g * 4:g * 4 + 4, :], [], [wk])
                        for c in range(4):
                            f = g * 4 + c
                            for j in range(8):
                                fw.op('pe', [wk, 'yt'], ['ps%d' % j], lambda: nc.tensor.matmul(
                                    psb[j][:, :], wt[:, c, j * 128:(j + 1) * 128], yt[:, f, :],
                                    start=(f == 0), stop=(f == 7)))
                    for j in range(8):
                        gcol = modT[:, 5 * 8 + j, cj:cj + 1]
                        fw.op('dve', ['ps%d' % j, 'modT', 'xt'], ['xt'], lambda: nc.vector.scalar_tensor_tensor(
                            xt[:, j, :], psb[j][:, :], gcol, xt[:, j, :], ALU.mult, ALU.add))

                if first:
                    adaln()
                    for tile in range(NTILE):
                        cj = cj_of(tile)
                        load_x(l == 0, tile)
                        ffn(0, 0, 2, cj, 0.5)
                        store_x(tile)
                        proj(tile, cj)
                else:
                    for tile in range(NTILE):
                        cj = cj_of(tile)
                        load_x(False, tile)
                        mix_out(tile, cj)
                        ffn(1, 2, 8, cj, 0.5)
                        if l < L - 1:
                            store_x(tile)
                        else:
                            rms_rstd()
                            fw.op('dve', ['xt', 'rstd'], ['sq'], lambda: nc.vector.tensor_tensor(
                                sq[:], xt[:], rstd[:].unsqueeze(1).to_broadcast([128, 8, TT]), ALU.mult))
                            fw.op('dve', ['sq', 'fnw'], ['sq'], lambda: nc.vector.tensor_tensor(
                                sq[:], sq[:], fnw[:].unsqueeze(2).to_broadcast([128, 8, TT]), ALU.mult))
                            fw.dma('sp', yT_out.rearrange("(k p) t -> p k t", p=128)[:, :, tile * TT:(tile + 1) * TT],
                                   sq[:], ['sq'], ['yout'])
                fw.barrier()

        def zero_rows(msb, r0, r1):
            z = msb("z", [128, 512])
            fw.op('pool', [], ['z'], lambda: nc.gpsimd.memset(z[:], 0.0))
            for tile in range(NTILE):
                for k in range(r0 // 128, r1 // 128):
                    fw.dma('sp', Ys[k * 128:(k + 1) * 128, tile * TT:(tile + 1) * TT], z[:], ['z'], [('Y', tile)])

        def attention_phase(l):
            with ExitStack() as ms:
                msb = mk_sb(ms)
                allP = [('P', t) for t in range(NTILE)]
                allY = [('Y', t) for t in range(NTILE)]
                cosT = msb("cosT", [64, 2048]); sinT = msb("sinT", [64, 2048])
                rotT = msb("rotT", [64, 64]); amask = msb("amask", [128, 3, 384]); sink = msb("sink", [128, 8])
                fw.dma('sp', cosT[:], cos_c[:, :], [], ['cosT'])
                fw.dma('sp', sinT[:], sin_c[:, :], [], ['sinT'])
                fw.dma('sp', rotT[:], rotT_c[:, :], [], ['rotT'])
                fw.dma('sp', amask[:], amask_c[:, :, :], [], ['amask'])
                fw.dma('sp', sink[:], sinkb[l], [], ['sink'])
                kpad = msb("kpad", [64, 2304]); kraw = msb("kraw", [64, 2048]); vraw = msb("vraw", [64, 2048])
                qraw = msb("qraw", [64, 2048]); qr = msb("qr", [64, 2048]); tmp = msb("tmp", [64, 512])
                vtok = msb("vtok", [128, 16, 64]); kctxT = msb("kctxT", [64, 512])
                cktok = msb("cktok", [128, 4, 64]); vctx = msb("vctx", [128, 4, 64])
                ybuf = msb("ybuf", [64, 2048])
                scr = Ring([msb("sc%d" % i, [128, 904]) for i in range(4)], 'sc')
                prr = Ring([msb("pr%d" % i, [128, 904]) for i in range(4)], 'pr')
                pTr = Ring([msb("pT%d" % i, [128, 7, 128]) for i in range(4)], 'pT')
                st = Ring([msb("st%d" % i, [128, 4]) for i in range(4)], 'st')
                fw.op('pool', [], ['kpad'], lambda: nc.gpsimd.memset(kpad[:], 0.0))

                def rope(src, skey, dst_ap, dkey, T):
                    for c in range(T // 512):
                        sl = slice(c * 512, (c + 1) * 512)
                        ps, pk = PS.next()
                        fw.op('pe', [skey, 'rotT'], [pk], lambda: nc.tensor.matmul(
                            ps[0:64, :], rotT[:], src[:, sl], start=True, stop=True))
                        fw.op('dve', [pk, 'sinT'], ['tmp'], lambda: nc.vector.tensor_tensor(
                            tmp[:], ps[0:64, :], sinT[:, sl], ALU.mult))
                        fw.op('pool', [skey, 'cosT'], [dkey], lambda: nc.gpsimd.tensor_tensor(
                            dst_ap[:, sl], src[:, sl], cosT[:, sl], ALU.mult))
                        fw.op('dve', ['tmp', dkey], [dkey], lambda: nc.vector.tensor_tensor(
                            dst_ap[:, sl], dst_ap[:, sl], tmp[:], ALU.add))

                def transposes_to(src_fn, nblk, dst_fn, rkeys, wkey, pin, pout):
                    for b0 in range(0, nblk, 4):
                        nb = min(4, nblk - b0)
                        ps, pk = PS.next()
                        for b in range(nb):
                            fw.op('pe', rkeys + ['ident'], [pk], lambda: nc.tensor.transpose(
                                ps[0:pout, b * pin:(b + 1) * pin], src_fn(b0 + b), ident[0:pin, 0:pin]))
                        fw.op('act', [pk], [wkey], lambda: nc.scalar.copy(
                            dst_fn(b0, nb), ps[0:pout, 0:nb * pin].rearrange("p (b i) -> p b i", i=pin)))

                def softmax_pv(qlhsT, qkey, parts, hq, vblocks, out_ap, okey):
                    sc, sk = scr.next()
                    s4, s4k = st.next()
                    pr, prk = prr.next()
                    pT, pTk = pTr.next()
                    off = 0
                    for (rhs, rkey, ncol, mask) in parts:
                        ps, pk = PS.next()
                        fw.op('pe', [qkey, rkey], [pk], lambda: nc.tensor.matmul(
                            ps[:, 0:ncol], qlhsT, rhs, start=True, stop=True))
                        yield
                        if mask is not None:
                            fw.op('dve', [pk, 'amask'], [sk], lambda: nc.vector.scalar_tensor_tensor(
                                sc[:, off:off + ncol], ps[:, 0:ncol], 0.125, mask, ALU.mult, ALU.add))
                        else:
                            fw.op('act', [pk], [sk], lambda: nc.scalar.mul(sc[:, off:off + ncol], ps[:, 0:ncol], 0.125))
                        off += ncol
                    ntot = off
                    fw.op('pool', ['sink'], [sk], lambda: nc.gpsimd.tensor_copy(sc[:, ntot:ntot + 1], sink[:, hq:hq + 1]))
                    yield
                    fw.op('dve', [sk], [s4k], lambda: nc.vector.reduce_max(
                        s4[:, 0:1], sc[:, 0:ntot + 1], mybir.AxisListType.X))
                    yield
                    fw.op('dve', [s4k], [s4k], lambda: nc.vector.tensor_scalar(
                        s4[:, 1:2], s4[:, 0:1], -1.0, None, ALU.mult))
                    yield
                    fw.op('act', [sk, s4k], [prk], lambda: nc.scalar.activation(
                        pr[:, 0:ntot + 1], sc[:, 0:ntot + 1], AF.Exp, bias=s4[:, 1:2], scale=1.0))
                    yield
                    fw.op('dve', [prk], [s4k], lambda: nc.vector.reduce_sum(
                        s4[:, 2:3], pr[:, 0:ntot + 1], mybir.AxisListType.X))
                    yield
                    fw.op('dve', [s4k], [s4k], lambda: nc.vector.reciprocal(s4[:, 3:4], s4[:, 2:3]))
                    yield
                    fw.op('dve', [prk, s4k], [prk], lambda: nc.vector.tensor_scalar(
                        pr[:, 0:ntot], pr[:, 0:ntot], s4[:, 3:4], None, ALU.mult))
                    yield
                    nblk = ntot // 128
                    transposes_to(lambda b: pr[:, b * 128:(b + 1) * 128], nblk,
                                  lambda b0, nb: pT[:, b0:b0 + nb, :], [prk], pTk, 128, 128)
                    yield
                    po, pok = PS.next()
                    for i, (b, vap, vk) in enumerate(vblocks):
                        fw.op('pe', [pTk, vk], [pok], lambda: nc.tensor.matmul(
                            po[0:64, 0:128], vap, pT[:, b, :], start=(i == 0), stop=(i == len(vblocks) - 1)))
                    yield
                    fw.op('act', [pok], [okey], lambda: nc.scalar.copy(out_ap, po[0:64, 0:128]))

                def run_interleaved(gens, width=2):
                    live = []
                    gens = list(gens)
                    while gens or live:
                        while gens and len(live) < width:
                            live.append(gens.pop(0))
                        nxt = []
                        for g_ in live:
                            try:
                                next(g_)
                                nxt.append(g_)
                            except StopIteration:
                                pass
                        live = nxt

                for s in range(2):
                    c0 = s * 256
                    for g in range(2):
                        fw.dma('sp', kraw[:, 0:256], Ps[1536 + g * 64:1600 + g * 64, c0:c0 + 256], [('P', 0)], ['kraw'])
                        fw.dma('sp', vraw[:, 0:256], Ps[1664 + g * 64:1728 + g * 64, c0:c0 + 256], [('P', 0)], ['vraw'])
                        transposes_to(lambda b: vraw[:, b * 128:(b + 1) * 128], 2,
                                      lambda b0, nb: vtok[:, b0:b0 + nb, :], ['vraw'], 'vtok', 64, 128)
                        for gi in range(4):
                            hq = g * 4 + gi
                            fw.dma('sp', qraw[:, 0:256], Ps[1024 + hq * 64:1088 + hq * 64, c0:c0 + 256],
                                   [('P', 0)], ['qraw'])
                            run_interleaved([softmax_pv(qraw[:, n * 128:(n + 1) * 128], 'qraw',
                                                        [(kraw[:, 0:256], 'kraw', 256, None)], hq,
                                                        [(b, vtok[:, b, :], 'vtok') for b in range(2)],
                                                        ybuf[:, n * 128:(n + 1) * 128], 'ybuf') for n in range(2)])
                            fw.dma('sp', Ys[256 + hq * 64:320 + hq * 64, c0:c0 + 256], ybuf[:, 0:256],
                                   ['ybuf'], [('Y', 0)])
                sP = allP[1:]
                for g in range(2):
                    fw.dma('sp', kraw[:], Ps[1536 + g * 64:1600 + g * 64, 512:NT], sP, ['kraw'])
                    rope(kraw, 'kraw', kpad[:, 128:2176], 'kpad', 2048)
                    fw.dma('sp', vraw[:], Ps[1664 + g * 64:1728 + g * 64, 512:NT], sP, ['vraw'])
                    transposes_to(lambda b: vraw[:, b * 128:(b + 1) * 128], 16,
                                  lambda b0, nb: vtok[:, b0:b0 + nb, :], ['vraw'], 'vtok', 64, 128)
                    fw.dma('sp', cktok[:], ck_in[l].rearrange("(b p) f -> p b f", p=128)[:, :, g * 64:(g + 1) * 64],
                           [], ['cktok'])
                    fw.dma('sp', vctx[:], cv_in[l].rearrange("(b p) f -> p b f", p=128)[:, :, g * 64:(g + 1) * 64],
                           [], ['vctx'])
                    transposes_to(lambda b: cktok[:, b, :], 4,
                                  lambda b0, nb: kctxT[:, b0 * 128:(b0 + nb) * 128].rearrange("p (b i) -> p b i", i=128),
                                  ['cktok'], 'kctxT', 128, 64)
                    for gi in range(4):
                        hq = g * 4 + gi
                        fw.dma('sp', qraw[:], Ps[1024 + hq * 64:1088 + hq * 64, 512:NT], sP, ['qraw'])
                        rope(qraw, 'qraw', qr, 'qr', 2048)
                        def items():
                            for n in range(16):
                                var = 0 if n == 0 else (2 if n == 15 else 1)
                                vbl = [(b, vtok[:, n - 1 + b, :], 'vtok') for b in range(3) if 0 <= n - 1 + b < 16]
                                vbl += [(3 + b, vctx[:, b, :], 'vctx') for b in range(4)]
                                yield softmax_pv(qr[:, n * 128:(n + 1) * 128], 'qr',
                                                 [(kpad[:, n * 128:n * 128 + 384], 'kpad', 384, amask[:, var, :]),
                                                  (kctxT[:, :], 'kctxT', 512, None)], hq, vbl,
                                                 ybuf[:, n * 128:(n + 1) * 128], 'ybuf')
                        run_interleaved(items(), 4)
                        fw.dma('sp', Ys[256 + hq * 64:320 + hq * 64, 512:NT], ybuf[:], ['ybuf'], allY[1:])
                fw.barrier()

        def scan_phase(l, which):
            with ExitStack() as ms:
                msb = mk_sb(ms)
                TM = 2048
                big = lambda name: msb(name, [128, TM])
                R_, K_, V_, A_, B_ = big("R"), big("K"), big("V"), big("A"), big("B")
                CS, CSX = big("CS"), big("CSX")
                t1, t2 = big("t1"), big("t2")
                padA, padB = msb("padA", [128, TM + 4]), msb("padB", [128, TM + 4])
                X1, X2 = big("X1"), big("X2")
                osc = msb("osc", [64, 2, TM])
                GT = msb("GT", [64, TM]); BON = msb("BON", [64, TM]) if which == 'rwkv' else None
                csrow = msb("csrow", [64, 2, TM]) if which == 'dn' else None
                prm = msb("prm", [128, 32]); prm2 = msb("prm2", [64, 4])
                w2c = msb("w2c", [128, 128]); a2c = msb("a2c", [128, 128]); g2h = msb("g2h", [128, 64])
                S = msb("S", [128, 64]); wendc = msb("wendc", [128, 4]); cendc = msb("cendc", [128, 4])
                negA = msb("negA", [128, 2]); negA2 = msb("negA2", [64, 2])
                mstr = msb("mstr", [64, 512]); minc = msb("minc", [64, 512]); id8 = msb("id8", [64, 512])
                fw.dma('sp', mstr[:], mstr_c[:, :], [], ['mstr'])
                fw.dma('sp', minc[:], minc_c[:, :], [], ['minc'])
                fw.dma('sp', id8[:], id8_c[:, :], [], ['id8'])
                bt = {}
                for nm in ("Wi", "Wx", "Winv", "Wrem", "oR", "oA", "oB", "oK", "oBh", "oKh"):
                    bt[nm] = msb("b" + nm, [128, 256])
                cc = {}
                for nm in ("N", "NT", "Lak", "Mrb", "Mrk", "X", "XT", "Pm", "DTs", "DTi", "LakV", "U0", "U"):
                    cc[nm] = msb("c" + nm, [64, 512])
                tk = {}
                for nm in ("At", "Bh", "Kh", "V"):
                    tk[nm] = msb("k" + nm, [64, 4, 128])
                AhT = msb("AhT", [128, 4, 64])
                ncol = msb("ncol", [64, 8])
                fw.op('pool', [], ['padA'], lambda: nc.gpsimd.memset(padA[:], 0.0))
                fw.op('pool', [], ['padB'], lambda: nc.gpsimd.memset(padB[:], 0.0))

                def pieces(T):
                    return [(p * 512, min(512, T - p * 512)) for p in range((T + 511) // 512)]

                def zpad(pad, padk, T, off):
                    fw.op('pool', [], [padk], lambda: nc.gpsimd.memset(pad[:, 0:off], 0.0))
                    fw.op('pool', [], [padk], lambda: nc.gpsimd.memset(pad[:, off + T:off + T + off], 0.0))

                def load_dup(pad, padk, row0, c0, T, off):
                    zpad(pad, padk, T, off)
                    for d in range(2):
                        fw.dma('sp', pad[d * 64:(d + 1) * 64, off:off + T], Ps[row0:row0 + 64, c0:c0 + T],
                               allP, [padk])

                def sum_halves(src, skey, T, dst, dkey, mat, func, bias, scale):
                    for (o, n) in pieces(T):
                        ps, pk = PS.next()
                        fw.op('pe', [skey, 'ones', 'onesb'], [pk], lambda: nc.tensor.matmul(
                            ps[:, 0:n], mat, src[:, o:o + n], start=True, stop=True))
                        fw.op('act', [pk], [dkey], lambda: nc.scalar.activation(
                            dst[:, o:o + n], ps[:, 0:n], func, bias=bias, scale=scale))

                def engine(T, mode, s0_ap, fin_ap):
                    nb = T // 256
                    stage = SCAN_TEST[1] if SCAN_TEST else 9
                    if stage < 2:
                        fw.op('pool', [], ['osc'], lambda: nc.gpsimd.memset(osc[:], 0.0))
                        return
                    if s0_ap is None:
                        fw.op('pool', [], ['S'], lambda: nc.gpsimd.memset(S[:], 0.0))
                    else:
                        fw.dma('sp', S[:], s0_ap, [], ['S'])
                    for b in range(nb):
                        sl = slice(b * 256, (b + 1) * 256)
                        v3 = lambda t: t[:, sl].rearrange("p (c i) -> p c i", i=64)
                        fw.op('dve', ['t1'], ['CS'], lambda: nc.vector.tensor_tensor_scan(
                            CS[:, sl], t1[:, sl], t1[:, sl], 0.0, ALU.add, ALU.bypass))
                        fw.op('dve', ['CS', 't1'], ['CSX'], lambda: nc.vector.tensor_tensor(
                            CSX[:, sl], CS[:, sl], t1[:, sl], ALU.subtract))
                        fw.op('dve', ['CSX'], ['cendc'], lambda: nc.vector.tensor_copy(
                            cendc[:, :].unsqueeze(2), v3(CSX)[:, :, 0:1]))
                        fw.op('dve', ['CS', 'cendc'], ['CS'], lambda: nc.vector.tensor_tensor(
                            v3(CS), v3(CS), cendc[:, :].unsqueeze(2).to_broadcast([128, 4, 64]), ALU.subtract))
                        fw.op('dve', ['CS', 't1'], ['CSX'], lambda: nc.vector.tensor_tensor(
                            CSX[:, sl], CS[:, sl], t1[:, sl], ALU.subtract))
                        fw.op('dve', ['CS'], ['cendc'], lambda: nc.vector.tensor_copy(
                            cendc[:, :].unsqueeze(2), v3(CS)[:, :, 63:64]))
                        fw.op('act', ['cendc'], ['wendc'], lambda: nc.scalar.activation(wendc[:], cendc[:], AF.Exp))
                        Wi, Wx, Winv, Wrem = bt["Wi"], bt["Wx"], bt["Winv"], bt["Wrem"]
                        fw.op('act', ['CS'], ['Wi'], lambda: nc.scalar.activation(Wi[:], CS[:, sl], AF.Exp))
                        fw.op('dve', ['CS', 'cendc'], ['Wrem'], lambda: nc.vector.tensor_tensor(
                            Wrem[:].rearrange("p (c i) -> p c i", i=64),
                            cendc[:, :].unsqueeze(2).to_broadcast([128, 4, 64]), v3(CS), ALU.subtract))
                        fw.op('act', ['Wrem'], ['Wrem'], lambda: nc.scalar.activation(Wrem[:], Wrem[:], AF.Exp))
                        if SCAN_SUB <= 10:
                            continue
                        mul = lambda e, out, okey, a_, akey, b_, bkey: fw.op(
                            e, [akey, bkey], [okey],
                            lambda: (nc.vector if e == 'dve' else nc.gpsimd).tensor_tensor(out, a_, b_, ALU.mult))
                        mul('dve', bt["oBh"][:], 'oBh', B_[:, sl], 'B', Wrem[:], 'Wrem')
                        mul('pool', bt["oKh"][:], 'oKh', K_[:, sl], 'K', Wrem[:], 'Wrem')
                        mul('dve', bt["oR"][:], 'oR', R_[:, sl], 'R', Wi[:], 'Wi')
                        if mode == 'vec':
                            fw.op('act', ['CSX'], ['Wx'], lambda: nc.scalar.activation(Wx[:], CSX[:, sl], AF.Exp))
                            fw.op('act', ['CS'], ['Winv'], lambda: nc.scalar.activation(
                                Winv[:], CS[:, sl], AF.Exp, scale=-1.0))
                            mul('pool', bt["oA"][:], 'oA', A_[:, sl], 'A', Wx[:], 'Wx')
                            mul('dve', bt["oB"][:], 'oB', B_[:, sl], 'B', Winv[:], 'Winv')
                            mul('pool', bt["oK"][:], 'oK', K_[:, sl], 'K', Winv[:], 'Winv')
                            Rop, Aop, Bop, Kop = (bt["oR"], 'oR', 0), (bt["oA"], 'oA', 0), (bt["oB"], 'oB', 0), (bt["oK"], 'oK', 0)
                            DTs, DTsk, DTi, DTik = mstr, 'mstr', minc, 'minc'
                        else:
                            mul('pool', bt["oA"][:], 'oA', A_[:, sl], 'A', Wi[:], 'Wi')
                            o0 = b * 256
                            Rop, Aop, Bop, Kop = (R_, 'R', o0), (A_, 'A', o0), (B_, 'B', o0), (K_, 'K', o0)
                            ps, pk = PS.next()
                            for c in range(4):
                                for d in range(2):
                                    fw.op('pe', ['csrow', 'onehot'], [pk], lambda: nc.tensor.matmul(
                                        ps[0:64, c * 2 + d:c * 2 + d + 1],
                                        csrow[:, d, o0 + c * 64:o0 + (c + 1) * 64], onehot[0:64, 0:1],
                                        start=True, stop=True))
                            fw.op('act', [pk], ['ncol'], lambda: nc.scalar.mul(ncol[:], ps[0:64, 0:8], -1.0))
                            fw.op('dve', ['csrow', 'ncol'], ['DTs'], lambda: nc.vector.tensor_tensor(
                                cc["DTs"][:].rearrange("p (c d i) -> p c d i", d=2, i=64),
                                csrow[:, :, o0:o0 + 256].rearrange("p d (c i) -> p c d i", i=64),
                                ncol[:].rearrange("p (c d) -> p c d", d=2).unsqueeze(3).to_broadcast([64, 4, 2, 64]),
                                ALU.add))
                            fw.op('dve', ['DTs'], ['DTs'], lambda: nc.vector.tensor_scalar(
                                cc["DTs"][:], cc["DTs"][:], 0.0, None, ALU.min))
                            fw.op('act', ['DTs'], ['DTs'], lambda: nc.scalar.activation(cc["DTs"][:], cc["DTs"][:], AF.Exp))
                            fw.op('pool', ['DTs', 'minc'], ['DTi'], lambda: nc.gpsimd.tensor_tensor(
                                cc["DTi"][:], cc["DTs"][:], minc[:], ALU.mult))
                            fw.op('dve', ['DTs', 'mstr'], ['DTs'], lambda: nc.vector.tensor_tensor(
                                cc["DTs"][:], cc["DTs"][:], mstr[:], ALU.mult))
                            DTs, DTsk, DTi, DTik = cc["DTs"], 'DTs', cc["DTi"], 'DTi'
                        At, Atk, Rt, Rtk = bt["oA"], 'oA', bt["oR"], 'oR'
                        if SCAN_SUB <= 11:
                            continue

                        def cxc(L_, R2, mask, mkey, dst, dkey):
                            ps, pk = PS.next()
                            (lt, lk, lo), (rt, rk, ro) = L_, R2
                            for d in range(2):
                                for c in range(4):
                                    fw.op('pe', [lk, rk], [pk], lambda: nc.tensor.matmul(
                                        ps[0:64, (c * 2 + d) * 64:(c * 2 + d + 1) * 64],
                                        lt[d * 64:(d + 1) * 64, lo + c * 64:lo + (c + 1) * 64],
                                        rt[d * 64:(d + 1) * 64, ro + c * 64:ro + (c + 1) * 64], start=True, stop=True), rt=d)
                            fw.op('dve', [pk, mkey], [dkey], lambda: nc.vector.tensor_tensor(
                                dst[:], ps[0:64, :], mask[:], ALU.mult))
                        cxc(Bop, Aop, DTs, DTsk, cc["N"], 'N')
                        same_bk = (mode == 'scal')
                        cLak = (lambda: fw.op('pool', ['N'], ['Lak'], lambda: nc.gpsimd.tensor_copy(cc["Lak"][:], cc["N"][:]))) \
                            if same_bk else (lambda: cxc(Kop, Aop, DTs, DTsk, cc["Lak"], 'Lak'))

                        def mm8(lhs, lkey, rhs, rkey, evac):
                            ps, pk = PS.next()
                            for m in range(8):
                                fw.op('pe', [lkey, rkey], [pk], lambda: nc.tensor.matmul(
                                    ps[0:64, m * 64:(m + 1) * 64], lhs(m), rhs(m), start=True, stop=True), inc=(m == 7))
                            evac(ps, pk)
                        blk = lambda t: (lambda m: t[:, m * 64:(m + 1) * 64])
                        cp = lambda dst, dkey: (lambda ps, pk: fw.op('act', [pk], [dkey], lambda: nc.scalar.copy(dst[:], ps[0:64, :])))
                        ps, pk = PS.next()
                        for m in range(8):
                            fw.op('pe', ['N', 'ident'], [pk], lambda: nc.tensor.transpose(
                                ps[0:64, m * 64:(m + 1) * 64], cc["N"][:, m * 64:(m + 1) * 64], ident[0:64, 0:64]))
                        cp(cc["NT"], 'NT')(ps, pk)
                        fw.op('dve', ['N', 'id8'], ['Pm'], lambda: nc.vector.tensor_tensor(cc["Pm"][:], cc["N"][:], id8[:], ALU.add))
                        cLak()

                        def tok_major(nm, src, skey, so):
                            ps, pk = PS.next()
                            for c in range(4):
                                fw.op('pe', [skey, 'ident'], [pk], lambda: nc.tensor.transpose(
                                    ps[0:64, c * 128:(c + 1) * 128], src[:, so + c * 64:so + (c + 1) * 64], ident[:, :]))
                            fw.op('dve', [pk], ['k' + nm], lambda: nc.vector.tensor_copy(
                                tk[nm][:], ps[0:64, :].rearrange("p (c f) -> p c f", f=128)))
                        fill = [lambda: None,
                                lambda: tok_major("V", V_, 'V', b * 256),
                                lambda: cxc(Bop, Rop, DTi, DTik, cc["Mrb"], 'Mrb'),
                                lambda: tok_major("At", At, Atk, 0),
                                (lambda: fw.op('pool', ['Mrb'], ['Mrk'], lambda: nc.gpsimd.tensor_copy(cc["Mrk"][:], cc["Mrb"][:])))
                                if same_bk else (lambda: cxc(Kop, Rop, DTi, DTik, cc["Mrk"], 'Mrk')),
                                lambda: tok_major("Bh", bt["oBh"], 'oBh', 0),
                                lambda: tok_major("Kh", bt["oKh"], 'oKh', 0)]
                        bufX = [(cc["N"], 'N'), (cc["X"], 'X')]
                        bufXT = [(cc["NT"], 'NT'), (cc["XT"], 'XT')]
                        padd = lambda ps, pk: fw.op('dve', [pk, 'Pm'], ['Pm'], lambda: nc.vector.tensor_tensor(
                            cc["Pm"][:], cc["Pm"][:], ps[0:64, :], ALU.add))
                        pend = None
                        for lev in range(5):
                            (Xc, Xk), (XTc, XTk) = bufX[lev % 2], bufXT[lev % 2]
                            (nX, nXk), (nXT, nXTk) = bufX[(lev + 1) % 2], bufXT[(lev + 1) % 2]
                            mm8(blk(Xc), Xk, blk(XTc), XTk, cp(nXT, nXTk))
                            if lev < 4:
                                mm8(blk(XTc), XTk, blk(Xc), Xk, cp(nX, nXk))
                            if pend is not None:
                                mm8(blk(pend[0]), pend[1], blk(cc["Pm"]), 'Pm', padd)
                            if fill:
                                fill.pop(0)()
                            if fill and lev % 2 == 0:
                                fill.pop(0)()
                            pend = (nXT, nXTk)
                        mm8(blk(pend[0]), pend[1], blk(cc["Pm"]), 'Pm', padd)
                        while fill:
                            fill.pop(0)()
                        TTm = cc["Pm"]
                        vt = lambda m: tk["V"][:, m // 2, (m % 2) * 64:(m % 2 + 1) * 64]
                        mm8(blk(cc["Lak"]), 'Lak', vt, 'kV', cp(cc["LakV"], 'LakV'))
                        mm8(blk(TTm), 'Pm', blk(cc["LakV"]), 'LakV', cp(cc["U0"], 'U0'))
                        if SCAN_SUB <= 16:
                            continue
                        ps, pk = PS.next()
                        for c in range(4):
                            fw.op('pe', ['kAt', 'Pm'], [pk], lambda: nc.tensor.matmul(
                                ps[:, c * 128:(c + 1) * 128], tk["At"][:, c, :], TTm[:, c * 128:(c + 1) * 128],
                                start=True, stop=True))
                        for d in range(2):
                            e = 'act' if d == 0 else 'dve'
                            src_ = ps[d * 64:(d + 1) * 64, :].rearrange("p (c d i) -> p c d i", d=2, i=64)[:, :, d, :]
                            if d == 0:
                                fw.op('act', [pk], ['AhT'], lambda: nc.scalar.copy(AhT[0:64, :, :], src_))
                            else:
                                fw.op('dve', [pk], ['AhT'], lambda: nc.vector.tensor_copy(AhT[64:128, :, :], src_))
                        if stage < 3:
                            fw.op('pool', [], ['osc'], lambda: nc.gpsimd.memset(osc[:], 0.0))
                        for c in range(4 if stage >= 3 else 0):
                            col = b * 256 + c * 64
                            ps, pk = PS.next()
                            for d in range(2):
                                fw.op('pe', ['AhT', 'S'], [pk], lambda: nc.tensor.matmul(
                                    ps[0:64, d * 64:(d + 1) * 64], AhT[d * 64:(d + 1) * 64, c, :],
                                    S[d * 64:(d + 1) * 64, :], start=True, stop=True), rt=d)
                            fw.op('dve', [pk, 'U0'], ['U'], lambda: nc.vector.tensor_tensor(
                                cc["U"][:, 0:128], ps[0:64, 0:128], cc["U0"][:, c * 128:(c + 1) * 128], ALU.add))
                            po, pok = PS.next()
                            for d in range(2):
                                m = c * 2 + d
                                if d == 0:
                                    fw.op('pe', ['S', Rtk], [pok], lambda: nc.tensor.matmul(
                                        po[0:64, 0:64], S[0:64, :], Rt[0:64, c * 64:(c + 1) * 64], start=True, stop=False))
                                fw.op('pe', ['U', 'Mrb'], [pok], lambda: nc.tensor.matmul(
                                    po[0:64, d * 64:(d + 1) * 64], cc["U"][:, d * 64:(d + 1) * 64],
                                    cc["Mrb"][:, m * 64:(m + 1) * 64], start=(d == 1), stop=False))
                                fw.op('pe', ['kV', 'Mrk'], [pok], lambda: nc.tensor.matmul(
                                    po[0:64, d * 64:(d + 1) * 64], vt(m),
                                    cc["Mrk"][:, m * 64:(m + 1) * 64], start=False, stop=(d == 0)))
                            fw.op('pe', ['S', Rtk], [pok], lambda: nc.tensor.matmul(
                                po[0:64, 64:128], S[64:128, :], Rt[64:128, c * 64:(c + 1) * 64], start=False, stop=True), rt=1)
                            fw.op('act', [pok], ['osc'], lambda: nc.scalar.copy(
                                osc[:, :, col:col + 64], po[0:64, 0:128].rearrange("p (d i) -> p d i", i=64)))
                            pS, pSk = PS.next()
                            fw.op('pe', ['kBh', 'U'], [pSk], lambda: nc.tensor.matmul(
                                pS[:, 0:128], tk["Bh"][:, c, :], cc["U"][:, 0:128], start=True, stop=False))
                            fw.op('pe', ['kKh', 'kV'], [pSk], lambda: nc.tensor.matmul(
                                pS[:, 0:128], tk["Kh"][:, c, :], tk["V"][:, c, :], start=False, stop=True))
                            for d in range(2):
                                fw.op('dve', [pSk, 'wendc', 'S'], ['S'], lambda: nc.vector.scalar_tensor_tensor(
                                    S[d * 64:(d + 1) * 64, :], S[d * 64:(d + 1) * 64, :], wendc[d * 64:(d + 1) * 64, c:c + 1],
                                    pS[d * 64:(d + 1) * 64, d * 64:(d + 1) * 64], ALU.mult, ALU.add))
                    if fin_ap is not None:
                        fw.dma('sp', fin_ap, S[:], ['S'], ['stout'])

                def unit_rwkv(c0, T, h, s0_ap, fin_ap, ycols):
                    fw.dma('sp', prm[:, 0:12], rwp_in[l, h], [], ['prm'])
                    fw.dma('sp', w2c[0:64, :], w2cat_in[l, h], [], ['w2c'])
                    fw.dma('sp', a2c[64:128, :], a2cat_in[l, h], [], ['a2c'])
                    fw.dma('sp', g2h[:], g2_in[l][:, h * 64:(h + 1) * 64], [], ['g2h'])
                    H0, H1 = slice(0, 64), slice(64, 128)

                    def shift(pad, padk, mucol, dst, dkey, rev):
                        x = pad[:, 1:T + 1]
                        fw.op('dve', [padk], ['t2'], lambda: nc.vector.tensor_tensor(
                            t2[:, 0:T], pad[:, 0:T], pad[:, 2:T + 2], ALU.add))
                        fw.op('dve', ['t2', padk], ['t2'], lambda: nc.vector.scalar_tensor_tensor(
                            t2[:, 0:T], t2[:, 0:T], 0.5, x, ALU.mult, ALU.subtract))
                        if not rev:
                            fw.op('dve', ['t2', padk, 'prm'], [dkey], lambda: nc.vector.scalar_tensor_tensor(
                                dst[:, 0:T], t2[:, 0:T], mucol, x, ALU.mult, ALU.add))
                        else:
                            fw.op('dve', ['t2', padk, 'prm'], [dkey], lambda: nc.vector.scalar_tensor_tensor(
                                dst[H0, 0:T], t2[H0, 0:T], mucol[H0], pad[H0, 1:T + 1], ALU.mult, ALU.add))
                            fw.op('dve', ['t2', padk, 'prm'], [dkey], lambda: nc.vector.scalar_tensor_tensor(
                                dst[H1, 0:T], t2[H1, 0:T][:, ::-1], mucol[H1], pad[H1, 1:T + 1][:, ::-1], ALU.mult, ALU.add))
                    load_dup(padA, 'padA', h * 64, c0, T, 1)
                    shift(padA, 'padA', prm[:, 0:1], R_, 'R', True)
                    if SCAN_SUB <= 1:
                        return
                    load_dup(padB, 'padB', 256 + h * 64, c0, T, 1)
                    shift(padB, 'padB', prm[:, 1:2], K_, 'K', True)
                    load_dup(padA, 'padA', 512 + h * 64, c0, T, 1)
                    shift(padA, 'padA', prm[:, 2:3], V_, 'V', True)
                    if SCAN_SUB <= 2:
                        return
                    zpad(padB, 'padB', T, 1)
                    fw.dma('sp', padB[:, 1:T + 1], Ps[768:896, c0:c0 + T], allP, ['padB'])
                    shift(padB, 'padB', prm[:, 10:11], X1, 'X1', False)
                    fw.op('act', ['X1'], ['X1'], lambda: nc.scalar.activation(X1[H0, 0:T], X1[H0, 0:T], AF.Tanh))
                    if SCAN_SUB <= 3:
                        return
                    for (o, n) in pieces(T):
                        ro = T - o - n
                        ps, pk = PS.next()
                        fw.op('pe', ['X1', 'w2c'], [pk], lambda: nc.tensor.matmul(
                            ps[:, 0:n], w2c[0:64, :], X1[H0, o:o + n], start=True, stop=True))
                        fw.op('act', [pk, 'prm'], ['t2'], lambda: nc.scalar.activation(
                            t2[:, o:o + n], ps[:, 0:n], AF.Sigmoid, bias=prm[:, 3:4], scale=1.0))
                        ps2, pk2 = PS.next()
                        fw.op('pe', ['X1', 'a2c'], [pk2], lambda: nc.tensor.matmul(
                            ps2[:, 0:n], a2c[64:128, :], X1[H1, o:o + n], start=True, stop=True), rt=1)
                        fw.op('act', [pk2, 'prm'], ['CSX'], lambda: nc.scalar.activation(
                            CSX[:, o:o + n], ps2[:, 0:n], AF.Sigmoid, bias=prm[:, 4:5], scale=1.0))
                    fw.op('dve', ['t2'], ['t1'], lambda: nc.vector.tensor_scalar(
                        t1[H0, 0:T], t2[H0, 0:T], -0.6065306597126334, None, ALU.mult))
                    fw.op('dve', ['t2'], ['t1'], lambda: nc.vector.tensor_scalar(
                        t1[H1, 0:T], t2[H1, 0:T][:, ::-1], -0.6065306597126334, None, ALU.mult))
                    fw.op('dve', ['CSX'], ['X2'], lambda: nc.vector.tensor_copy(X2[H0, 0:T], CSX[H0, 0:T]))
                    fw.op('dve', ['CSX'], ['X2'], lambda: nc.vector.tensor_copy(X2[H1, 0:T], CSX[H1, 0:T][:, ::-1]))
                    if SCAN_SUB <= 4:
                        return
                    zpad(padA, 'padA', T, 1)
                    fw.dma('sp', padA[:, 1:T + 1], Ps[896:1024, c0:c0 + T], allP, ['padA'])
                    shift(padA, 'padA', prm[:, 11:12], X1, 'X1', False)
                    fw.op('act', ['X1'], ['X1'], lambda: nc.scalar.activation(X1[:, 0:T], X1[:, 0:T], AF.Sigmoid))
                    for (o, n) in pieces(T):
                        ps, pk = PS.next()
                        fw.op('pe', ['X1', 'g2h'], [pk], lambda: nc.tensor.matmul(
                            ps[0:64, 0:n], g2h[:], X1[:, o:o + n], start=True, stop=True))
                        fw.op('act', [pk], ['GT'], lambda: nc.scalar.copy(GT[:, o:o + n], ps[0:64, 0:n]))
                    if SCAN_SUB <= 5:
                        return
                    fw.op('dve', ['K', 'prm'], ['A'], lambda: nc.vector.tensor_scalar(
                        A_[:, 0:T], K_[:, 0:T], prm[:, 5:6], None, ALU.mult))
                    fw.op('pool', ['A'], ['X1'], lambda: nc.gpsimd.tensor_tensor(X1[:, 0:T], A_[:, 0:T], A_[:, 0:T], ALU.mult))
                    sum_halves(X1, 'X1', T, B_, 'B', onesb[:], AF.Sqrt, EPS, 1.0)
                    fw.op('dve', ['B'], ['B'], lambda: nc.vector.reciprocal(B_[:, 0:T], B_[:, 0:T]))
                    fw.op('dve', ['A', 'B'], ['A'], lambda: nc.vector.tensor_tensor(A_[:, 0:T], A_[:, 0:T], B_[:, 0:T], ALU.mult))
                    fw.op('dve', ['A', 'X2'], ['B'], lambda: nc.vector.tensor_tensor(B_[:, 0:T], A_[:, 0:T], X2[:, 0:T], ALU.mult))
                    fw.op('dve', ['A'], ['A'], lambda: nc.vector.tensor_scalar(A_[:, 0:T], A_[:, 0:T], -1.0, None, ALU.mult))
                    if SCAN_SUB <= 6:
                        return
                    fw.op('dve', ['X2', 'prm'], ['X2'], lambda: nc.vector.tensor_scalar(
                        X2[:, 0:T], X2[:, 0:T], -1.0, prm[:, 6:7], ALU.add, ALU.mult))
                    fw.op('dve', ['X2', 'K'], ['K'], lambda: nc.vector.scalar_tensor_tensor(
                        K_[:, 0:T], X2[:, 0:T], 1.0, K_[:, 0:T], ALU.add, ALU.mult))
                    fw.op('dve', ['R', 'K'], ['X1'], lambda: nc.vector.tensor_tensor(X1[:, 0:T], R_[:, 0:T], K_[:, 0:T], ALU.mult))
                    fw.op('dve', ['X1', 'prm'], ['X2'], lambda: nc.vector.tensor_scalar(
                        X2[H0, 0:T], X1[H0, 0:T], prm[H0, 7:8], None, ALU.mult))
                    fw.op('dve', ['X1', 'prm'], ['X2'], lambda: nc.vector.tensor_scalar(
                        X2[H1, 0:T], X1[H1, 0:T][:, ::-1], prm[H1, 7:8], None, ALU.mult))
                    for (o, n) in pieces(T):
                        ps, pk = PS.next()
                        fw.op('pe', ['X2', 'ones'], [pk], lambda: nc.tensor.matmul(
                            ps[0:64, 0:n], ones[:, 0:64], X2[:, o:o + n], start=True, stop=True))
                        fw.op('dve', [pk, 'V'], ['BON'], lambda: nc.vector.tensor_tensor(
                            BON[:, o:o + n], ps[0:64, 0:n], V_[H0, o:o + n], ALU.mult))
                    if SCAN_SUB <= 7:
                        return
                    engine(T, 'vec', s0_ap, fin_ap)
                    o_ = X1
                    fw.op('dve', ['osc'], ['X1'], lambda: nc.vector.tensor_tensor(
                        o_[H0, 0:T], osc[:, 0, 0:T], osc[:, 1, 0:T][:, ::-1], ALU.add))
                    for (o, n) in pieces(T):
                        ps, pk = PS.next()
                        fw.op('pe', ['X1', 'ones'], [pk], lambda: nc.tensor.matmul(
                            ps[0:64, 0:n], ones[0:64, 0:64], o_[H0, o:o + n], start=True, stop=True))
                        fw.op('dve', [pk, 'X1'], ['X2'], lambda: nc.vector.scalar_tensor_tensor(
                            X2[H0, o:o + n], ps[0:64, 0:n], -1.0 / 64, o_[H0, o:o + n], ALU.mult, ALU.add))
                        fw.op('pool', ['X2'], ['t2'], lambda: nc.gpsimd.tensor_tensor(
                            t2[H0, o:o + n], X2[H0, o:o + n], X2[H0, o:o + n], ALU.mult))
                        ps2, pk2 = PS.next()
                        fw.op('pe', ['t2', 'ones'], [pk2], lambda: nc.tensor.matmul(
                            ps2[0:64, 0:n], ones[0:64, 0:64], t2[H0, o:o + n], start=True, stop=True))
                        fw.op('act', [pk2], ['t2'], lambda: nc.scalar.activation(
                            t2[H0, o:o + n], ps2[0:64, 0:n], AF.Sqrt, bias=64e-5, scale=1.0 / 64))
                    fw.op('dve', ['t2'], ['t2'], lambda: nc.vector.reciprocal(t2[H0, 0:T], t2[H0, 0:T]))
                    fw.op('dve', ['X2', 't2'], ['X2'], lambda: nc.vector.tensor_tensor(X2[H0, 0:T], X2[H0, 0:T], t2[H0, 0:T], ALU.mult))
                    fw.op('dve', ['X2', 'prm'], ['X2'], lambda: nc.vector.tensor_scalar(
                        X2[H0, 0:T], X2[H0, 0:T], prm[H0, 8:9], prm[H0, 9:10], ALU.mult, ALU.add))
                    fw.op('dve', ['X2', 'BON'], ['X2'], lambda: nc.vector.tensor_tensor(X2[H0, 0:T], X2[H0, 0:T], BON[:, 0:T], ALU.add))
                    fw.op('dve', ['X2', 'GT'], ['X2'], lambda: nc.vector.tensor_tensor(X2[H0, 0:T], X2[H0, 0:T], GT[:, 0:T], ALU.mult))
                    fw.dma('sp', Ys[h * 64:(h + 1) * 64, c0:c0 + T], X2[H0, 0:T], ['X2'], ycols)

                def unit_dn(c0, T, h, s0_ap, fin_ap, ycols):
                    fw.dma('sp', prm[:, 0:20], dnp_in[l, h], [], ['prm'])
                    fw.dma('sp', prm2[:, :], dnp2_in[l, h], [], ['prm2'])
                    H0, H1 = slice(0, 64), slice(64, 128)
                    fw.op('act', ['prm'], ['negA'], lambda: nc.scalar.activation(negA[:, 0:1], prm[:, 16:17], AF.Exp))
                    fw.op('dve', ['negA'], ['negA'], lambda: nc.vector.tensor_scalar(negA[:, 1:2], negA[:, 0:1], -1.0, None, ALU.mult))
                    fw.op('act', ['prm2'], ['negA2'], lambda: nc.scalar.activation(negA2[:, 0:2], prm2[:, 0:2], AF.Exp))
                    fw.op('dve', ['negA2'], ['negA2'], lambda: nc.vector.tensor_scalar(negA2[:, 0:2], negA2[:, 0:2], -1.0, None, ALU.mult))

                    def conv(pad, padk, w0, dst, dkey):
                        fw.op('dve', [padk, 'prm'], ['t2'], lambda: nc.vector.tensor_scalar(
                            t2[:, 0:T], pad[:, 0:T], prm[:, w0:w0 + 1], None, ALU.mult))
                        for j in range(1, 5):
                            fw.op('dve', [padk, 'prm', 't2'], ['t2'], lambda: nc.vector.scalar_tensor_tensor(
                                t2[:, 0:T], pad[:, j:j + T], prm[:, w0 + j:w0 + j + 1], t2[:, 0:T], ALU.mult, ALU.add))
                        fw.op('act', ['t2'], [dkey], lambda: nc.scalar.activation(dst[:, 0:T], t2[:, 0:T], AF.Silu))

                    def l2n(src, skey, dst, dkey, scale):
                        fw.op('pool', [skey], ['t2'], lambda: nc.gpsimd.tensor_tensor(t2[:, 0:T], src[:, 0:T], src[:, 0:T], ALU.mult))
                        sum_halves(t2, 't2', T, CSX, 'CSX', onesb[:], AF.Sqrt, EPS, 1.0)
                        fw.op('dve', ['CSX'], ['CSX'], lambda: nc.vector.reciprocal(CSX[:, 0:T], CSX[:, 0:T]))
                        fw.op('dve', [skey, 'CSX'], [dkey], lambda: nc.vector.scalar_tensor_tensor(
                            dst[H0, 0:T], src[H0, 0:T], scale, CSX[H0, 0:T], ALU.mult, ALU.mult))
                        fw.op('dve', [skey, 'CSX'], [dkey], lambda: nc.vector.scalar_tensor_tensor(
                            dst[H1, 0:T], src[H1, 0:T][:, ::-1], scale, CSX[H1, 0:T][:, ::-1], ALU.mult, ALU.mult))
                    load_dup(padA, 'padA', 1792 + h * 64, c0, T, 2)
                    conv(padA, 'padA', 0, X1, 'X1')
                    l2n(X1, 'X1', R_, 'R', 0.125)
                    load_dup(padB, 'padB', 2048 + h * 64, c0, T, 2)
                    conv(padB, 'padB', 5, X1, 'X1')
                    l2n(X1, 'X1', K_, 'K', 1.0)
                    load_dup(padA, 'padA', 2304 + h * 64, c0, T, 2)
                    conv(padA, 'padA', 10, X1, 'X1')
                    for d in range(2):
                        fw.dma('sp', padB[d * 64:(d + 1) * 64, 0:T],
                               Ps[2816 + d * 4 + h:2816 + d * 4 + h + 1, c0:c0 + T].to_broadcast([64, T]), allP, ['padB'])
                        fw.dma('sp', X2[d * 64:(d + 1) * 64, 0:T],
                               Ps[2824 + d * 4 + h:2824 + d * 4 + h + 1, c0:c0 + T].to_broadcast([64, T]), allP, ['X2'])
                        fw.dma('sp', csrow[:, d, 0:T],
                               Ps[2816 + d * 4 + h:2816 + d * 4 + h + 1, c0:c0 + T].to_broadcast([64, T]), allP, ['csrow'])
                    fw.op('act', ['padB', 'prm'], ['t2'], lambda: nc.scalar.activation(
                        t2[:, 0:T], padB[:, 0:T], AF.Exp, bias=prm[:, 17:18], scale=1.0))
                    fw.op('act', ['t2'], ['t2'], lambda: nc.scalar.activation(t2[:, 0:T], t2[:, 0:T], AF.Ln, bias=1.0, scale=1.0))
                    fw.op('dve', ['t2', 'negA'], ['t1'], lambda: nc.vector.tensor_scalar(
                        t1[H0, 0:T], t2[H0, 0:T], negA[H0, 1:2], None, ALU.mult))
                    fw.op('dve', ['t2', 'negA'], ['t1'], lambda: nc.vector.tensor_scalar(
                        t1[H1, 0:T], t2[H1, 0:T][:, ::-1], negA[H1, 1:2], None, ALU.mult))
                    for d in range(2):
                        fw.op('act', ['csrow', 'prm2'], ['csrow'], lambda: nc.scalar.activation(
                            csrow[:, d, 0:T], csrow[:, d, 0:T], AF.Exp, bias=prm2[:, 2 + d:3 + d], scale=1.0))
                        fw.op('act', ['csrow'], ['csrow'], lambda: nc.scalar.activation(
                            csrow[:, d, 0:T], csrow[:, d, 0:T], AF.Ln, bias=1.0, scale=1.0))
                    fw.op('dve', ['csrow', 'negA2'], ['t2'], lambda: nc.vector.tensor_scalar(
                        t2[0:64, 0:T], csrow[:, 1, 0:T], negA2[:, 1:2], None, ALU.mult))
                    fw.op('dve', ['csrow', 'negA2'], ['csrow'], lambda: nc.vector.tensor_scalar(
                        csrow[:, 0, 0:T], csrow[:, 0, 0:T], negA2[:, 0:1], None, ALU.mult))
                    fw.op('dve', ['t2'], ['csrow'], lambda: nc.vector.tensor_copy(csrow[:, 1, 0:T], t2[0:64, 0:T][:, ::-1]))
                    for d in range(2):
                        for b in range(T // 256):
                            sl = slice(b * 256, (b + 1) * 256)
                            fw.op('dve', ['csrow'], ['t2'], lambda: nc.vector.tensor_tensor_scan(
                                t2[0:64, sl], csrow[:, d, sl], csrow[:, d, sl], 0.0, ALU.add, ALU.bypass))
                            fw.op('dve', ['csrow', 't2'], ['cendc'], lambda: nc.vector.tensor_tensor(
                                cendc[0:64, :].unsqueeze(2), t2[0:64, sl].rearrange("p (c i) -> p c i", i=64)[:, :, 0:1],
                                csrow[:, d, sl].rearrange("p (c i) -> p c i", i=64)[:, :, 0:1], ALU.subtract))
                            fw.op('dve', ['cendc', 't2'], ['csrow'], lambda: nc.vector.tensor_tensor(
                                csrow[:, d, sl].rearrange("p (c i) -> p c i", i=64),
                                t2[0:64, sl].rearrange("p (c i) -> p c i", i=64),
                                cendc[0:64, :].unsqueeze(2).to_broadcast([64, 4, 64]), ALU.subtract))
                    fw.op('act', ['X2'], ['X2'], lambda: nc.scalar.activation(X2[:, 0:T], X2[:, 0:T], AF.Sigmoid))
                    fw.op('dve', ['X2', 'X1'], ['V'], lambda: nc.vector.tensor_tensor(V_[H0, 0:T], X1[H0, 0:T], X2[H0, 0:T], ALU.mult))
                    fw.op('dve', ['X2', 'X1'], ['V'], lambda: nc.vector.tensor_tensor(
                        V_[H1, 0:T], X1[H1, 0:T][:, ::-1], X2[H1, 0:T][:, ::-1], ALU.mult))
                    fw.op('dve', ['X2'], ['CSX'], lambda: nc.vector.tensor_copy(CSX[H0, 0:T], X2[H0, 0:T]))
                    fw.op('dve', ['X2'], ['CSX'], lambda: nc.vector.tensor_copy(CSX[H1, 0:T], X2[H1, 0:T][:, ::-1]))
                    fw.op('dve', ['CSX', 'K'], ['A'], lambda: nc.vector.scalar_tensor_tensor(
                        A_[:, 0:T], CSX[:, 0:T], -1.0, K_[:, 0:T], ALU.mult, ALU.mult))
                    fw.op('pool', ['K'], ['B'], lambda: nc.gpsimd.tensor_copy(B_[:, 0:T], K_[:, 0:T]))
                    fw.dma('sp', GT[:, 0:T], Ps[2560 + h * 64:2624 + h * 64, c0:c0 + T], allP, ['GT'])
                    fw.op('act', ['GT'], ['GT'], lambda: nc.scalar.activation(GT[:, 0:T], GT[:, 0:T], AF.Silu))
                    engine(T, 'scal', s0_ap, fin_ap)
                    o_ = X1
                    fw.op('dve', ['osc'], ['X1'], lambda: nc.vector.tensor_tensor(
                        o_[H0, 0:T], osc[:, 0, 0:T], osc[:, 1, 0:T][:, ::-1], ALU.add))
                    fw.op('pool', ['X1'], ['t2'], lambda: nc.gpsimd.tensor_tensor(t2[H0, 0:T], o_[H0, 0:T], o_[H0, 0:T], ALU.mult))
                    for (o, n) in pieces(T):
                        ps, pk = PS.next()
                        fw.op('pe', ['t2', 'ones'], [pk], lambda: nc.tensor.matmul(
                            ps[0:64, 0:n], ones[0:64, 0:64], t2[H0, o:o + n], start=True, stop=True))
                        fw.op('act', [pk], ['X2'], lambda: nc.scalar.activation(
                            X2[H0, o:o + n], ps[0:64, 0:n], AF.Sqrt, bias=EPS, scale=1.0 / 64))
                    fw.op('dve', ['X2'], ['X2'], lambda: nc.vector.reciprocal(X2[H0, 0:T], X2[H0, 0:T]))
                    fw.op('dve', ['X2', 'X1', 'prm'], ['X2'], lambda: nc.vector.scalar_tensor_tensor(
                        X2[H0, 0:T], o_[H0, 0:T], prm[H0, 15:16], X2[H0, 0:T], ALU.mult, ALU.mult))
                    fw.op('dve', ['X2', 'GT'], ['X2'], lambda: nc.vector.tensor_tensor(X2[H0, 0:T], X2[H0, 0:T], GT[:, 0:T], ALU.mult))
                    fw.dma('sp', Ys[768 + h * 64:832 + h * 64, c0:c0 + T], X2[H0, 0:T], ['X2'], ycols)

                allP = [('P', t) for t in range(NTILE)]
                unit = unit_rwkv if which == 'rwkv' else unit_dn
                mi = 0 if which == 'rwkv' else 1
                st_in = srw_in if which == 'rwkv' else sdn_in
                if SCAN_TEST:
                    unit(0, 256, 1, None, st_out[l, 0, mi, 1], [('Y', 0)])
                    if SCAN_TEST[1] >= 5:
                        unit(512, 2048, 2, st_in[l, 2], None, [('Y', t) for t in range(1, NTILE)])
                else:
                    for s in range(2):
                        for h in range(4):
                            unit(s * 256, 256, h, None, st_out[l, s, mi, h], [('Y', 0)])
                    for h in range(4):
                        unit(512, 2048, h, st_in[l, h], None, [('Y', t) for t in range(1, NTILE)])
                fw.barrier()

        def mixers(l):
            with ExitStack() as ms:
                msb = mk_sb(ms)
                if not ENABLE_RWKV:
                    zero_rows(msb, 0, 256)
                if not ENABLE_ATTN:
                    zero_rows(msb, 256, 768)
                if not ENABLE_DN:
                    zero_rows(msb, 768, 1024)
                fw.barrier()
            if ENABLE_ATTN:
                attention_phase(l)
            if ENABLE_RWKV:
                scan_phase(l, 'rwkv')
            if ENABLE_DN:
                scan_phase(l, 'dn')
            if DEBUG and l == 0:
                fw.dma('sp', dbgY[:, :], Ys[:, :], [('Y', t) for t in range(NTILE)], ['dbgY'])
                fw.dma('sp', dbgP[:, :], Ps[:, :], [('P', t) for t in range(NTILE)], ['dbgP'])

        if SCAN_TEST:
            for t in range(NTILE):
                fw.dma('sp', Ps[:, t * TT:(t + 1) * TT], ptest[:, t * TT:(t + 1) * TT], [], [('P', t)])
                fw.dma('sp', Ys[:, t * TT:(t + 1) * TT], ptest[0:D, t * TT:(t + 1) * TT], [], [('Y', t)])
            fw.barrier()
            scan_phase(0, SCAN_TEST[0])
            fw.dma('sp', yT_out[:, :], Ys[:, :], [('Y', t) for t in range(NTILE)], ['yout'])
        for l in range(0 if SCAN_TEST else N_LAYERS_BUILD):
            trunk_phase(l, True)
            mixers(l)
            trunk_phase(l, False)
        fw.finish('sp')
        print("instructions:", fw.nins, fw.cnt)
    return nc


_NC = None


def _rope_tables():
    t = np.arange(2048)
    row = (t // 64).astype(np.float32)
    col = (t % 64).astype(np.float32)
    freqs = (10000.0 ** (-np.arange(16, dtype=np.float32) / 16)).astype(np.float32)
    cos = np.zeros((64, 2048), np.float32)
    sin = np.zeros((64, 2048), np.float32)
    for d in range(64):
        pos = row if d < 32 else col
        ang = (pos * freqs[d % 16]).astype(np.float32)
        cos[d] = np.cos(ang)
        sin[d] = np.sin(ang)
    R = np.zeros((64, 64), np.float32)
    for base in (0, 32):
        for i in range(16):
            R[base + i, base + i + 16] = -1.0
            R[base + 16 + i, base + i] = 1.0
    return cos, sin, np.ascontiguousarray(R.T)


def _attn_masks():
    qi = np.arange(128)[:, None]
    kj = np.arange(384)[None, :]
    band = np.abs(kj - 128 - qi) <= 128
    m = np.zeros((128, 3, 384), np.float32)
    for var, n in ((0, 0), (1, 5), (2, 15)):
        key_pos = (n - 1) * 128 + kj
        ok = band & (key_pos >= 0) & (key_pos < 2048)
        m[:, var, :] = np.where(ok, 0.0, NEG)
    return m


def kernel(x_prompt, x_sample, cache_attn_k, cache_attn_v, state_rwkv, state_delta, c, c_ctx,
           norm_w, ada_w, ada_b, ffn_w_in, ffn_w_out, w_in, w_out,
           rwkv_mu, rwkv_w0, rwkv_w2, rwkv_a0, rwkv_a2, rwkv_g2, rwkv_kk, rwkv_ka, rwkv_rk,
           rwkv_lnx_w, rwkv_lnx_b, attn_sink, dn_conv, dn_A_log, dn_dt_bias, dn_norm_w, final_norm_w):
    global _NC
    f = lambda a: np.ascontiguousarray(np.asarray(a, dtype=np.float32))
    x_prompt, x_sample = f(x_prompt), f(x_sample)
    cache_attn_k, cache_attn_v = f(cache_attn_k), f(cache_attn_v)
    if _NC is None:
        _NC = build_program()
    nc = _NC
    fm = lambda v: f(np.asarray(v).reshape(-1, 128).T)
    cos, sin, rotT = _rope_tables()
    shared = {
        "ada_w": f(ada_w),
        "ada_bT": f(np.stack([fm(ada_b[l]) for l in range(L)])),
        "norm_wT": f(np.stack([np.concatenate([fm(norm_w[l, n]) for n in range(3)], axis=1) for l in range(L)])),
        "fnorm_wT": fm(final_norm_w),
        "ffn_w_in": f(ffn_w_in), "ffn_w_out": f(ffn_w_out), "w_in": f(w_in), "w_out": f(w_out),
        "ones_c": np.ones((128, 128), np.float32),
        "ident_c": np.eye(128, dtype=np.float32),
        "cos_c": cos, "sin_c": sin, "rotT_c": rotT,
        "amask_c": _attn_masks(),
        "g2": f(rwkv_g2),
        "mstr_c": f(np.tile(np.triu(np.ones((64, 64), np.float32), 1), (1, 8))),
        "minc_c": f(np.tile(np.triu(np.ones((64, 64), np.float32), 0), (1, 8))),
        "id8_c": f(np.tile(np.eye(64, dtype=np.float32), (1, 8))),
        "onesb_c": f(np.kron(np.eye(2, dtype=np.float32), np.ones((64, 64), np.float32))),
        "onehot_c": f(np.eye(128, dtype=np.float32)[:, 0:1]),
        "sinkb": f(np.broadcast_to(np.asarray(attn_sink, np.float32)[:, None, :], (L, 128, 8))),
    }
    A = lambda v: np.asarray(v, np.float32)
    rwp = np.zeros((L, 4, 128, 12), np.float32)
    w2cat = np.zeros((L, 4, 64, 128), np.float32)
    a2cat = np.zeros((L, 4, 64, 128), np.float32)
    dnp = np.zeros((L, 4, 128, 20), np.float32)
    dnp2 = np.zeros((L, 4, 64, 4), np.float32)
    mu, w0, a0 = A(rwkv_mu), A(rwkv_w0), A(rwkv_a0)
    conv, Alog, dtb = A(dn_conv), A(dn_A_log), A(dn_dt_bias)
    for l in range(L):
        for h in range(4):
            js = slice(h * 64, (h + 1) * 64)
            for d in range(2):
                ps_ = slice(d * 64, (d + 1) * 64)
                rwp[l, h, ps_, 0] = mu[l, 0:256][js]
                rwp[l, h, ps_, 1] = mu[l, 256:512][js]
                rwp[l, h, ps_, 2] = mu[l, 512:768][js]
                rwp[l, h, ps_, 3] = w0[l, d][js]
                rwp[l, h, ps_, 4] = a0[l, d][js]
                rwp[l, h, ps_, 5] = A(rwkv_kk)[l][js]
                rwp[l, h, ps_, 6] = A(rwkv_ka)[l][js]
                rwp[l, h, ps_, 7] = A(rwkv_rk)[l][js]
                rwp[l, h, ps_, 8] = A(rwkv_lnx_w)[l][js]
                rwp[l, h, ps_, 9] = A(rwkv_lnx_b)[l][js]
                w2cat[l, h, :, ps_] = A(rwkv_w2)[l, d][:, js]
                a2cat[l, h, :, ps_] = A(rwkv_a2)[l, d][:, js]
                for t in range(5):
                    dnp[l, h, ps_, t] = conv[l, t, 0:256][js]
                    dnp[l, h, ps_, 5 + t] = conv[l, t, 256:512][js]
                    dnp[l, h, ps_, 10 + t] = conv[l, t, 512:768][js]
                dnp[l, h, ps_, 15] = A(dn_norm_w)[l]
                dnp[l, h, ps_, 16] = Alog[l, d, h]
                dnp[l, h, ps_, 17] = dtb[l, d, h]
                dnp2[l, h, :, d] = Alog[l, d, h]
                dnp2[l, h, :, 2 + d] = dtb[l, d, h]
            rwp[l, h, :, 10] = mu[l, 768:896]
            rwp[l, h, :, 11] = mu[l, 896:1024]
    shared.update({"rwp": rwp, "w2cat": w2cat, "a2cat": a2cat, "dnp": dnp, "dnp2": dnp2})
    state_rwkv, state_delta = A(state_rwkv), A(state_delta)
    in_maps = []
    for core in range(8):
        b = core % 2
        xs = np.concatenate([x_prompt[2 * core].T, x_prompt[2 * core + 1].T, x_sample[b].T], axis=1)
        condT = np.stack([fm(c_ctx), fm(np.asarray(c)[b])], axis=-1)
        m = dict(shared)
        m["xT"] = f(xs)
        m["condT"] = f(condT)
        m["ck"] = f(cache_attn_k[b].reshape(L, 512, 128))
        m["cv"] = f(cache_attn_v[b].reshape(L, 512, 128))
        m["srw"] = f(state_rwkv[b].transpose(0, 2, 1, 4, 3).reshape(L, 4, 128, 64))
        m["sdn"] = f(state_delta[b].transpose(0, 2, 1, 3, 4).reshape(L, 4, 128, 64))
        in_maps.append(m)
    if SCAN_TEST:
        for m in in_maps:
            for k_ in ("ffn_w_in", "ffn_w_out"):
                m[k_] = np.zeros((1, 1, 128, 128), np.float32)
            for k_ in ("ada_w", "w_in", "w_out"):
                m[k_] = np.zeros((1, 128, 128), np.float32)
    res = run_bass_kernel_spmd(nc, in_maps, core_ids=list(range(8)))
    r = res.results
    global _LAST
    _LAST = r
    y_prompt = np.stack([r[s // 2]["yT"][:, (s % 2) * 256:(s % 2 + 1) * 256].T for s in range(16)])
    y_sample = np.stack([r[b]["yT"][:, 512:].T for b in range(2)])
    new_k = np.zeros((16, L, 256, 2, 64), np.float32)
    new_v = np.zeros((16, L, 256, 2, 64), np.float32)
    for s in range(16):
        kv = r[s // 2]["kvT"][:, :, (s % 2) * 256:(s % 2 + 1) * 256]
        new_k[s] = kv[:, 0:128, :].transpose(0, 2, 1).reshape(L, 256, 2, 64)
        new_v[s] = kv[:, 128:256, :].transpose(0, 2, 1).reshape(L, 256, 2, 64)
    new_sr = np.zeros((16, L, 2, 4, 64, 64), np.float32)
    new_sd = np.zeros((16, L, 2, 4, 64, 64), np.float32)
    for s in range(16):
        st = r[s // 2]["st"][:, s % 2]
        st = st.reshape(L, 2, 4, 2, 64, 64)
        new_sr[s] = st[:, 0].transpose(0, 2, 1, 4, 3)
        new_sd[s] = st[:, 1].transpose(0, 2, 1, 3, 4)
    return (np.ascontiguousarray(y_prompt, dtype=np.float32), np.ascontiguousarray(y_sample, dtype=np.float32),
            new_k, new_v, new_sr, new_sd)
```
